# Optimizing a Trainium2 kernel written in Bass

```python
import jax
import jax.numpy as jnp
from jax import lax
import numpy as np

D_MODEL = 4096
BATCH = 8
SEQ = 2048
DEPTH = 2

HEAD_DIM = 128
N_HEADS_TOTAL = D_MODEL // HEAD_DIM
MLA_HEADS = N_HEADS_TOTAL // 4
MLA_Q_LORA = D_MODEL // 4
MLA_KV_LORA = D_MODEL // 8
MLA_NOPE_DIM = 128
MLA_ROPE_DIM = 64
MLA_V_DIM = 128
MLA_QK_DIM = MLA_NOPE_DIM + MLA_ROPE_DIM
ROPE_THETA = 10000.0
SWA_HEADS = N_HEADS_TOTAL // 2
SWA_KV_HEADS = max(1, SWA_HEADS // 8)
WINDOW = 128
FOX_HEADS = N_HEADS_TOTAL // 4
Q_BLOCK = 128
N_BRANCH = 3
FFN_HIDDEN = ((8 * D_MODEL // 3 + 255) // 256) * 256
RMS_EPS = 1e-6
MAX_POS_OFFSET = 1024

IN_SPLITS = (
    MLA_Q_LORA,
    MLA_KV_LORA,
    MLA_ROPE_DIM,
    SWA_HEADS * HEAD_DIM,
    SWA_KV_HEADS * HEAD_DIM,
    SWA_KV_HEADS * HEAD_DIM,
    FOX_HEADS * HEAD_DIM,
    FOX_HEADS * HEAD_DIM,
    FOX_HEADS * HEAD_DIM,
    FOX_HEADS,
    N_BRANCH * D_MODEL,
)
IN_WIDTH = sum(IN_SPLITS)
MLA_OUT = MLA_HEADS * MLA_V_DIM
SWA_OUT = SWA_HEADS * HEAD_DIM
FOX_OUT = FOX_HEADS * HEAD_DIM
MIX_WIDTH = MLA_OUT + SWA_OUT + FOX_OUT

kernel_name = "hybrid_mla_swa_fox_gated_block"


def rms_norm(x, gain):
    xf = x.astype(jnp.float32)
    y = xf * lax.rsqrt(jnp.mean(xf * xf, axis=-1, keepdims=True) + RMS_EPS)
    return (y * gain.astype(jnp.float32)).astype(x.dtype)


def split_points():
    pts, acc = [], 0
    for s in IN_SPLITS[:-1]:
        acc += s
        pts.append(acc)
    return pts


def apply_rope(x, positions):
    half = x.shape[-1] // 2
    inv_freq = ROPE_THETA ** (-jnp.arange(half, dtype=jnp.float32) / half)
    ang = positions.astype(jnp.float32)[..., None] * inv_freq
    cos = jnp.cos(ang)[:, :, None, :]
    sin = jnp.sin(ang)[:, :, None, :]
    xf = x.astype(jnp.float32)
    x1, x2 = xf[..., :half], xf[..., half:]
    out = jnp.concatenate([x1 * cos - x2 * sin, x2 * cos + x1 * sin], axis=-1)
    return out.astype(x.dtype)


def alibi_slopes(n_heads):
    return 2.0 ** (-8.0 * jnp.arange(1, n_heads + 1, dtype=jnp.float32) / n_heads)


def causal_block_attention(q, k, v, scale, log_decay=None):
    B, S, H, _ = q.shape
    nb = S // Q_BLOCK
    qb = jnp.moveaxis(q.reshape(B, nb, Q_BLOCK, H, -1), 1, 0)
    key_idx = jnp.arange(S)

    def attend(blk, qi, fq):
        s = jnp.einsum("bqhd,bshd->bhqs", qi, k).astype(jnp.float32) * scale
        if fq is not None:
            s = s + jnp.transpose(fq, (0, 2, 1))[..., None] - fk
        q_idx = blk * Q_BLOCK + jnp.arange(Q_BLOCK)
        s = jnp.where(key_idx[None, :] <= q_idx[:, None], s, -jnp.inf)
        p = jax.nn.softmax(s, axis=-1)
        return jnp.einsum("bhqs,bshd->bqhd", p.astype(v.dtype), v)

    blocks = jnp.arange(nb)
    if log_decay is None:
        out = lax.map(lambda a: attend(a[0], a[1], None), (blocks, qb))
    else:
        ld = log_decay.astype(jnp.float32)
        fk = jnp.transpose(ld, (0, 2, 1))[:, :, None, :]
        fqb = jnp.moveaxis(ld.reshape(B, nb, Q_BLOCK, H), 1, 0)
        out = lax.map(lambda a: attend(a[0], a[1], a[2]), (blocks, qb, fqb))
    return jnp.moveaxis(out, 0, 1).reshape(B, S, -1)


def sliding_window_attention(q, k, v, positions, sinks, slopes):
    B, S, H, Dh = q.shape
    KVH = k.shape[2]
    G = H // KVH
    nb = S // WINDOW
    qb = q.reshape(B, nb, WINDOW, KVH, G, Dh)

    def with_prev(t):
        tb = t.reshape((B, nb, WINDOW) + t.shape[2:])
        prev = jnp.concatenate([jnp.zeros_like(tb[:, :1]), tb[:, :-1]], axis=1)
        return jnp.concatenate([prev, tb], axis=2)

    kw, vw, pw = with_prev(k), with_prev(v), with_prev(positions)
    qpos = positions.reshape(B, nb, WINDOW)
    s = jnp.einsum("bnqkgd,bnskd->bnkgqs", qb, kw).astype(jnp.float32) * (Dh ** -0.5)
    i = jnp.arange(WINDOW)[:, None]
    j = jnp.arange(2 * WINDOW)[None, :]
    d_idx = i + WINDOW - j
    blk = jnp.arange(nb)[:, None, None]
    valid = (d_idx >= 0) & (d_idx < WINDOW) & ((blk > 0) | (j >= WINDOW))
    dist = (qpos[:, :, :, None] - pw[:, :, None, :]).astype(jnp.float32)
    m_h = slopes.reshape(KVH, G)[None, None, :, :, None, None]
    s = s - m_h * dist[:, :, None, None]
    s = jnp.where(valid[None, :, None, None], s, -jnp.inf)
    sink = sinks.astype(jnp.float32).reshape(KVH, G)[None, None, :, :, None, None]
    mx = jnp.maximum(jnp.max(s, axis=-1, keepdims=True), sink)
    e = jnp.exp(s - mx)
    p = e / (jnp.sum(e, axis=-1, keepdims=True) + jnp.exp(sink - mx))
    o = jnp.einsum("bnkgqs,bnskd->bnqkgd", p.astype(v.dtype), vw)
    return o.reshape(B, S, H * Dh)


def hybrid_mixer(h, positions, w_in, g_q_lora, w_uq, g_kv_lora, w_ukv, b_forget, swa_sinks, w_branch, w_out):
    B, S, _ = h.shape
    proj = h @ w_in
    (cq, ckv, kr, q_swa, k_swa, v_swa, q_fox, k_fox, v_fox, z_fox, gate_logits) = jnp.split(
        proj, split_points(), axis=-1)

    q = (rms_norm(cq, g_q_lora) @ w_uq).reshape(B, S, MLA_HEADS, MLA_QK_DIM)
    q = jnp.concatenate([q[..., :MLA_NOPE_DIM], apply_rope(q[..., MLA_NOPE_DIM:], positions)], axis=-1)
    kv = (rms_norm(ckv, g_kv_lora) @ w_ukv).reshape(B, S, MLA_HEADS, MLA_NOPE_DIM + MLA_V_DIM)
    k_nope, v = kv[..., :MLA_NOPE_DIM], kv[..., MLA_NOPE_DIM:]
    k_rope = apply_rope(kr[:, :, None, :], positions)
    k = jnp.concatenate([k_nope, jnp.broadcast_to(k_rope, (B, S, MLA_HEADS, MLA_ROPE_DIM))], axis=-1)
    o_mla = causal_block_attention(q, k, v, MLA_QK_DIM ** -0.5)

    o_swa = sliding_window_attention(
        q_swa.reshape(B, S, SWA_HEADS, HEAD_DIM),
        k_swa.reshape(B, S, SWA_KV_HEADS, HEAD_DIM),
        v_swa.reshape(B, S, SWA_KV_HEADS, HEAD_DIM),
        positions, swa_sinks, alibi_slopes(SWA_HEADS))

    log_f = jax.nn.log_sigmoid((z_fox + b_forget).astype(jnp.float32))
    cum_log_f = lax.cumsum(log_f, axis=1)
    o_fox = causal_block_attention(
        q_fox.reshape(B, S, FOX_HEADS, HEAD_DIM),
        k_fox.reshape(B, S, FOX_HEADS, HEAD_DIM),
        v_fox.reshape(B, S, FOX_HEADS, HEAD_DIM),
        HEAD_DIM ** -0.5, log_decay=cum_log_f)

    gates = jax.nn.sigmoid(gate_logits).reshape(B, S, N_BRANCH, D_MODEL)
    r1 = MLA_OUT
    r2 = MLA_OUT + SWA_OUT
    merged = (gates[:, :, 0] * (o_mla @ w_branch[:r1])
              + gates[:, :, 1] * (o_swa @ w_branch[r1:r2])
              + gates[:, :, 2] * (o_fox @ w_branch[r2:]))
    return merged @ w_out


def setup_inputs(seed: int = 0) -> dict:
    key = jax.random.key(seed)
    ks = jax.random.split(key, 20)
    L, D = DEPTH, D_MODEL

    def normal(k, shape, scale):
        return jax.random.normal(k, shape, jnp.float32) * scale

    def gain(k, n):
        return 1.0 + 0.05 * jax.random.normal(k, (L, n), jnp.float32)

    x = normal(ks[0], (BATCH, SEQ, D), 1.0)
    offsets = jax.random.randint(ks[1], (BATCH, 1), 0, MAX_POS_OFFSET, dtype=jnp.int32)
    positions = offsets + jnp.arange(SEQ, dtype=jnp.int32)[None, :]
    return {
        "x": x,
        "positions": positions,
        "g_mix_pre": gain(ks[2], D),
        "g_mix_post": gain(ks[3], D),
        "g_ffn_pre": gain(ks[4], D),
        "g_ffn_post": gain(ks[5], D),
        "w_in": normal(ks[6], (L, D, IN_WIDTH), D ** -0.5),
        "g_q_lora": gain(ks[7], MLA_Q_LORA),
        "w_uq": normal(ks[8], (L, MLA_Q_LORA, MLA_HEADS * MLA_QK_DIM), MLA_Q_LORA ** -0.5),
        "g_kv_lora": gain(ks[9], MLA_KV_LORA),
        "w_ukv": normal(ks[10], (L, MLA_KV_LORA, MLA_HEADS * (MLA_NOPE_DIM + MLA_V_DIM)), MLA_KV_LORA ** -0.5),
        "b_forget": 2.0 + 0.5 * jax.random.normal(ks[11], (L, FOX_HEADS), jnp.float32),
        "swa_sinks": normal(ks[12], (L, SWA_HEADS), 0.5),
        "w_branch": normal(ks[13], (L, MIX_WIDTH, D), MIX_WIDTH ** -0.5),
        "w_out": normal(ks[14], (L, D, D), D ** -0.5),
        "w_gate_up": normal(ks[15], (L, D, 2 * FFN_HIDDEN), D ** -0.5),
        "w_down": normal(ks[16], (L, FFN_HIDDEN, D), FFN_HIDDEN ** -0.5),
    }


def reference(x, positions, g_mix_pre, g_mix_post, g_ffn_pre, g_ffn_post, w_in, g_q_lora, w_uq,
              g_kv_lora, w_ukv, b_forget, swa_sinks, w_branch, w_out, w_gate_up, w_down):
    for l in range(DEPTH):
        h = rms_norm(x, g_mix_pre[l])
        m = hybrid_mixer(h, positions, w_in[l], g_q_lora[l], w_uq[l], g_kv_lora[l], w_ukv[l],
                         b_forget[l], swa_sinks[l], w_branch[l], w_out[l])
        x = x + rms_norm(m, g_mix_post[l])
        h = rms_norm(x, g_ffn_pre[l])
        gate, up = jnp.split(h @ w_gate_up[l], 2, axis=-1)
        x = x + rms_norm((jax.nn.silu(gate) * up) @ w_down[l], g_ffn_post[l])
    return x
```

```python
import contextlib
import numpy as np
import concourse.bass as bass
import concourse.mybir as mybir
from concourse.bass_utils import run_bass_kernel_spmd

F32 = mybir.dt.float32
BF16 = mybir.dt.bfloat16
I32 = mybir.dt.int32
ALU = mybir.AluOpType
AF = mybir.ActivationFunctionType
AX = mybir.AxisListType

SAME_ENGINE_SYNC = True


class Op:
    __slots__ = ("eng", "fn", "deps", "dma", "signal", "count", "dcount", "idx", "inc")


class Prog:
    STREAMS = ("pe", "act", "dve", "pool", "sp")

    def __init__(self):
        self.ops = []
        self.lastw = {}
        self.rd_eng = {}
        self.rd_dma = {}
        self.bar = set()
        self.bar_pending = set()
        self.since_dma = []
        self.last_eng = {}

    def barrier(self):
        self.bar = set(self.last_eng.values()) | set(self.since_dma)
        self.since_dma = []
        self.bar_pending = set(self.STREAMS)

    def add(self, eng, fn, r=(), w=(), dma=None, inc=16, nowaw=False):
        i = len(self.ops)
        deps = set()
        lw = self.lastw
        for k in r:
            j = lw.get(k)
            if j is not None:
                deps.add(j)
        for k in w:
            j = lw.get(k)
            if j is not None and not nowaw:
                deps.add(j)
            d = self.rd_eng.get(k)
            if d:
                deps.update(d.values())
            d = self.rd_dma.get(k)
            if d:
                deps.update(d)
        if eng in self.bar_pending:
            deps |= self.bar
            self.bar_pending.discard(eng)
        if dma is None:
            self.last_eng[eng] = i
        else:
            self.since_dma.append(i)
        for k in r:
            if dma is None:
                self.rd_eng.setdefault(k, {})[eng] = i
            else:
                self.rd_dma.setdefault(k, []).append(i)
        for k in w:
            lw[k] = i
            self.rd_eng[k] = {}
            self.rd_dma[k] = []
        op = Op()
        op.eng, op.fn, op.deps, op.dma, op.signal, op.idx = eng, fn, deps, dma, False, i
        op.count = 0
        op.inc = inc
        op.dcount = 0
        self.ops.append(op)
        return i

    def emit(self, nc):
        ops = self.ops
        for op in ops:
            for j in op.deps:
                d = ops[j]
                if d.dma is not None:
                    continue
                if d.eng == op.eng and op.dma is None and (op.eng == "pe" or not SAME_ENGINE_SYNC):
                    continue
                d.signal = True
        cnt = {s: 0 for s in self.STREAMS}
        dcnt = {}
        for op in ops:
            if op.dma is not None:
                dcnt[op.dma] = dcnt.get(op.dma, 0) + op.inc
                op.dcount = dcnt[op.dma]
            elif op.signal:
                cnt[op.eng] += 1
                op.count = cnt[op.eng]
        self.stats = dict(cnt=cnt, ndma_groups=len(dcnt), nops=len(ops))
        with contextlib.ExitStack() as st:
            esem = {s: st.enter_context(nc.semaphore("e_" + s)) for s in self.STREAMS}
            dsem = {g: st.enter_context(nc.semaphore("d_%d" % n)) for n, g in enumerate(dcnt)}
            block = st.enter_context(nc.Block())

            def run(name, eng):
                waited = {}
                for op in ops:
                    if op.eng != name:
                        continue
                    need = {}
                    for j in op.deps:
                        d = ops[j]
                        if d.dma is not None:
                            s, v = dsem[d.dma], d.dcount
                        elif d.eng == op.eng and op.dma is None and (op.eng == "pe" or not SAME_ENGINE_SYNC):
                            continue
                        else:
                            s, v = esem[d.eng], d.count
                        key = id(s)
                        if need.get(key, (None, 0))[1] < v:
                            need[key] = (s, v)
                    for key, (s, v) in need.items():
                        if waited.get(key, 0) < v:
                            eng.wait_ge(s, v)
                            waited[key] = v
                    ins = op.fn(eng)
                    if op.dma is not None:
                        ins.then_inc(dsem[op.dma], op.inc)
                    elif op.signal:
                        ins.then_inc(esem[op.eng], 1)
                if name == "sp":
                    for g, v in dcnt.items():
                        eng.wait_ge(dsem[g], v)

            block.tensor(lambda e: run("pe", e))
            block.scalar(lambda e: run("act", e))
            block.vector(lambda e: run("dve", e))
            block.gpsimd(lambda e: run("pool", e))
            block.sync(lambda e: run("sp", e))


class Arena:
    def __init__(self, nc, st, nbytes, name="arena"):
        self.t = st.enter_context(nc.sbuf_tensor(name, [128, nbytes // 2], BF16))
        self.nbytes = nbytes
        self.off = 0
        self.marks = []

    def alloc(self, shape_free, dtype, parts=128):
        esz = 4 if dtype in (F32, I32) else 2
        n = int(np.prod(shape_free))
        nb = (n * esz + 31) // 32 * 32
        assert self.off + nb <= self.nbytes, ("SBUF arena overflow", self.off, nb, self.nbytes)
        a = self.t[0:parts, self.off // 2:(self.off + nb) // 2]
        self.off += nb
        if esz == 4:
            a = a.bitcast(dtype)
        a = a[:, 0:n]
        if len(shape_free) == 2:
            a = a.rearrange("p (a b) -> p a b", b=shape_free[1])
        elif len(shape_free) == 3:
            a = a.rearrange("p (a b c) -> p a b c", b=shape_free[1], c=shape_free[2])
        return a

    def mark(self):
        return self.off

    def reset(self, m):
        self.off = m


class Cfg:
    def __init__(s, D=4096, T=2048, DEPTH=2, NSEQ=1):
        s.D, s.T, s.DEPTH, s.NSEQ = D, T, DEPTH, NSEQ
        NH = D // 128
        s.MLA_H, s.QL, s.KVL, s.ROPE = NH // 4, D // 4, D // 8, 64
        s.SWA_H = NH // 2
        s.SWA_KV = max(1, s.SWA_H // 8)
        s.FOX_H = NH // 4
        s.FFN = ((8 * D // 3 + 255) // 256) * 256
        s.splits = [s.QL, s.KVL, 64, s.SWA_H * 128, s.SWA_KV * 128, s.SWA_KV * 128,
                    s.FOX_H * 128, s.FOX_H * 128, s.FOX_H * 128, s.FOX_H, 3 * D]
        s.INW = sum(s.splits)
        s.KC, s.NT, s.TG = D // 128, T // 128, T // 512
        s.MLA_OUT, s.SWA_OUT, s.FOX_OUT = s.MLA_H * 128, s.SWA_H * 128, s.FOX_H * 128
        s.EPS = 1e-6


PI = float(np.pi)


class Builder:
    def __init__(s, cfg):
        s.c = cfg
        s.nc = bass.Bass("TRN2", target_bir_lowering=False)
        s.P = Prog()
        s.uid = 0
        s.slabctr = 0
        s.halfctr = 0

    def din(s, name, shape, dt=F32):
        return s.nc.dram_tensor(name, list(shape), dt, kind="ExternalInput").ap()

    def dscr(s, name, shape, dt):
        return s.nc.dram_tensor(name, list(shape), dt, kind="Internal").ap()

    def dma(s, q, out, in_, r=(), w=(), grp=None, nowaw=False, **kw):
        if grp is None:
            grp = "misc"
            r = list(r) + ["miscchain"]
            w = list(w) + ["miscchain"]
        s.P.add(q, lambda e: e.dma_start(out=out, in_=in_, **kw), r=r, w=w, dma=grp, nowaw=nowaw)

    def dbg(s, name, ap, shape, dt=F32):
        if not getattr(s.c, "DEBUG", False):
            return
        s.uid += 1
        t = s.nc.dram_tensor("dbg_%s_%d" % (name, s.uid), list(shape), dt, kind="ExternalOutput").ap()
        s.dma("sp", t, ap, r=[name])

    def op(s, eng, fn, r=(), w=()):
        s.P.add(eng, fn, r=r, w=w)

    def build(s):
        c, nc = s.c, s.nc
        D, T, L, NS = c.D, c.T, c.DEPTH, c.NSEQ
        KC, NT, TG = c.KC, c.NT, c.TG
        i = s.inp = {}
        i["xT"] = s.din("xT", [NS, D, T])
        i["posb"] = s.din("posb", [NS, 64, T], I32)
        i["posk"] = s.din("posk", [NS, 128, NT], I32)
        i["posr"] = s.din("posr", [NS, 128, NT], I32)
        i["posrow"] = s.din("posrow", [NS, 16, T], I32)
        i["posrrow"] = s.din("posrrow", [NS, 16, T], I32)
        for g in ("g_mix_pre", "g_mix_post", "g_ffn_pre", "g_ffn_post"):
            i[g] = s.din(g, [L, 128, KC])
        i["g_q_lora"] = s.din("g_q_lora", [L, 128, c.QL // 128])
        i["g_kv_lora"] = s.din("g_kv_lora", [L, 128, c.KVL // 128])
        i["b_forget"] = s.din("b_forget", [L, 128, c.FOX_H])
        i["swa_sinks"] = s.din("swa_sinks", [L, 16, 1])
        i["w_in"] = s.din("w_in", [L, D, c.INW])
        i["w_uq"] = s.din("w_uq", [L, c.QL, c.MLA_H * 192])
        i["w_ukv"] = s.din("w_ukv", [L, c.KVL, c.MLA_H * 256])
        i["w_branch"] = s.din("w_branch", [L, D, D])
        i["w_out"] = s.din("w_out", [L, D, D])
        i["w_gate_up"] = s.din("w_gate_up", [L, D, 2 * c.FFN])
        i["w_down"] = s.din("w_down", [L, c.FFN, D])
        i["cst_f"] = s.din("cst_f", [128, 128 * 4 + 64 + 2])
        i["cst_oh"] = s.din("cst_oh", [16, 16 * 128])
        s.out = nc.dram_tensor("out", [NS, D, T], F32, kind="ExternalOutput").ap()
        sc = s.sc = {}
        sc["cq"] = s.dscr("sc_cq", [c.QL, T], F32)
        sc["ckv"] = s.dscr("sc_ckv", [c.KVL, T], F32)
        sc["kr"] = s.dscr("sc_kr", [64, T], F32)
        sc["z"] = s.dscr("sc_z", [c.FOX_H, T], F32)
        for nm, rows in (("qswa", c.SWA_H * 128), ("kswa", c.SWA_KV * 128), ("vswa", c.SWA_KV * 128),
                         ("qfox", c.FOX_H * 128), ("kfox", c.FOX_H * 128), ("vfox", c.FOX_H * 128),
                         ("gates", 3 * D), ("qn", c.MLA_H * 128), ("qr", c.MLA_H * 64), ("kn", c.MLA_H * 128),
                         ("vm", c.MLA_H * 128), ("krr", 64), ("o", D), ("merged", D), ("a", c.FFN)):
            sc[nm] = s.dscr("sc_" + nm, [rows, T], BF16)
        sc["m"] = s.dscr("sc_m", [D, T], F32)
        sc["xa"] = s.dscr("sc_xa", [D, T], F32)
        sc["xb"] = s.dscr("sc_xb", [D, T], F32)
        with contextlib.ExitStack() as st:
            s.ar = Arena(nc, st, 206 * 1024)
            s.ps = st.enter_context(nc.psum_tensor("ps", [128, 8, 512], F32))
            s.consts()
            for sq in range(NS):
                s.seq_prep(sq)
                for l in range(L):
                    xin = i["xT"][sq] if l == 0 else sc["xb"]
                    xout = s.out[sq] if l == L - 1 else sc["xb"]
                    s.layer(sq, l, xin, sc["xa"], xout)
            s.P.emit(nc)
        return nc

    def consts(s):
        ar, P, i = s.ar, s.P, s.inp
        cf = ar.alloc([128 * 4 + 66], F32)
        s.dma("sp", cf, i["cst_f"], w=["cf"])
        s.identf = cf[:, 0:128]
        s.U = cf[:, 128:256]
        s.E64 = cf[:, 256:384]
        s.RT = cf[0:64, 512:576]
        s.invf = cf[0:64, 576:577]
        s.slopes = cf[0:16, 577:578]
        s.identb = ar.alloc([128], BF16)
        s.tri = ar.alloc([128], BF16)
        s.low = ar.alloc([128], BF16)
        s.onesb = ar.alloc([128], BF16)
        s.onesf = ar.alloc([128], F32)
        s.op("dve", lambda e: e.tensor_copy(out=s.identb, in_=s.identf), r=["cf"], w=["identb"])
        s.op("dve", lambda e: e.tensor_copy(out=s.tri, in_=cf[:, 384:512]), r=["cf"], w=["tri"])
        s.op("dve", lambda e: e.tensor_scalar(out=s.low, in0=cf[:, 384:512], scalar1=-1.0, scalar2=1.0,
                                              op0=ALU.mult, op1=ALU.add), r=["cf"], w=["low"])
        s.op("dve", lambda e: e.memset(s.onesb, 1.0), w=["onesb"])
        s.op("dve", lambda e: e.memset(s.onesf, 1.0), w=["onesf"])
        s.oh = ar.alloc([16, 128], BF16, parts=16)
        T, NT = s.c.T, s.c.NT
        s.dcur = ar.alloc([NT], F32)
        s.dprev = ar.alloc([NT], F32)
        s.posrel = ar.alloc([T], F32, parts=16)
        s.base = ar.mark()
        ohf = ar.alloc([16 * 128], F32, parts=16)
        s.dma("sp", ohf, i["cst_oh"], w=["ohf"])
        s.op("dve", lambda e: e.tensor_copy(out=s.oh.rearrange("p a b -> p (a b)"), in_=ohf), r=["ohf"], w=["oh"])
        s.sc["cos"] = s.dscr("sc_cos", [64, T], F32)
        s.sc["sin"] = s.dscr("sc_sin", [64, T], F32)
        s.sc["posrel"] = s.dscr("sc_posrel", [16, T], F32)

    def seq_prep(s, sq):
        c, ar, P, i = s.c, s.ar, s.P, s.inp
        T, NT = c.T, c.NT
        P.barrier()
        ar.reset(s.base)
        pi_ = ar.alloc([T], I32, parts=64)
        pf = ar.alloc([T], F32, parts=64)
        ang = ar.alloc([T], F32, parts=64)
        tmp = ar.alloc([T], F32, parts=64)
        s.dma("sp", pi_, i["posb"][sq], w=["pi"])
        s.op("dve", lambda e: e.tensor_copy(out=pf, in_=pi_), r=["pi"], w=["pf"])
        s.op("dve", lambda e: e.tensor_scalar(out=ang, in0=pf, scalar1=s.invf, scalar2=None, op0=ALU.mult),
             r=["pf", "cf"], w=["ang"])
        MAGIC = 12582912.0
        u = ar.alloc([T], F32, parts=64)
        t2 = ar.alloc([T], F32, parts=64)
        s.sinT = ar.alloc([T], F32, parts=64)
        s.cosT = ar.alloc([T], F32, parts=64)
        for tab, sh, nm in ((s.sinT, 0.0, "sinT"), (s.cosT, 0.25, "cosT")):
            s.op("dve", lambda e, sh=sh: e.tensor_scalar(out=u, in0=ang, scalar1=1.0 / (2 * PI), scalar2=sh,
                                                         op0=ALU.mult, op1=ALU.add), r=["ang"], w=["u"])
            s.op("dve", lambda e: e.tensor_scalar(out=tmp, in0=u, scalar1=MAGIC, scalar2=None, op0=ALU.add), r=["u"], w=["tmp"])
            s.op("dve", lambda e: e.tensor_scalar(out=t2, in0=tmp, scalar1=MAGIC, scalar2=None, op0=ALU.subtract), r=["tmp"], w=["t2"])
            s.op("dve", lambda e: e.tensor_tensor(out=u, in0=u, in1=t2, op=ALU.subtract), r=["u", "t2"], w=["u"])
            s.op("act", lambda e, tab=tab: e.activation(out=tab, in_=u, func=AF.Sin, scale=2 * PI), r=["u"], w=[nm])
            s.dbg(nm, tab, [64, T])
            s.dma("sp", s.sc["sin" if nm == "sinT" else "cos"], tab, r=[nm])
        pk = ar.alloc([NT], I32)
        pr = ar.alloc([NT], I32)
        pkf = ar.alloc([NT], F32)
        prf = ar.alloc([NT], F32)
        s.dma("sp", pk, i["posk"][sq], w=["pk"])
        s.dma("sp", pr, i["posr"][sq], w=["pr"])
        s.op("dve", lambda e: e.tensor_copy(out=pkf, in_=pk), r=["pk"], w=["pkf"])
        s.op("dve", lambda e: e.tensor_copy(out=prf, in_=pr), r=["pr"], w=["prf"])
        s.op("dve", lambda e: e.tensor_tensor(out=s.dcur, in0=pkf, in1=prf, op=ALU.subtract), r=["pkf", "prf"], w=["dcur"])
        s.op("dve", lambda e: e.memset(s.dprev[:, 0:1], 0.0), w=["dprev"])
        if NT > 1:
            s.op("dve", lambda e: e.tensor_tensor(out=s.dprev[:, 1:NT], in0=pkf[:, 0:NT - 1], in1=prf[:, 1:NT],
                                                  op=ALU.subtract), r=["pkf", "prf"], w=["dprev"])
        a1 = ar.alloc([T], I32, parts=16)
        a2 = ar.alloc([T], I32, parts=16)
        a1f = ar.alloc([T], F32, parts=16)
        a2f = ar.alloc([T], F32, parts=16)
        s.dma("sp", a1, i["posrow"][sq], w=["a1"])
        s.dma("sp", a2, i["posrrow"][sq], w=["a2"])
        s.op("dve", lambda e: e.tensor_copy(out=a1f, in_=a1), r=["a1"], w=["a1f"])
        s.op("dve", lambda e: e.tensor_copy(out=a2f, in_=a2), r=["a2"], w=["a2f"])
        s.op("dve", lambda e: e.tensor_tensor(out=s.posrel, in0=a1f, in1=a2f, op=ALU.subtract), r=["a1f", "a2f"], w=["posrel"])

    def psr(s, half, m, ntg):
        ap = s.ps[0:m, half * 4:half * 4 + ntg, :].rearrange("p a b -> p (a b)")
        return ap, [("ps", half * 4 + t) for t in range(ntg)]

    def sumsq_rstd(s, src, nch, nfeat, rstd, Tn=None, t0=0):
        c, ar = s.c, s.ar
        T = c.T
        mk = ar.mark()
        xt = [ar.alloc([T], F32) for _ in range(2)]
        sq = ar.alloc([T], F32)
        ssp = ar.alloc([T], F32)
        for ch in range(nch):
            b = xt[ch % 2]
            s.dma("sp", b, src[ch * 128:(ch + 1) * 128, :], w=[("xt", ch % 2)], grp=("xt", ch % 2))
            if ch == 0:
                s.op("act", lambda e, b=b: e.activation(out=ssp, in_=b, func=AF.Square), r=[("xt", 0)], w=["ssp"])
            else:
                s.op("act", lambda e, b=b: e.activation(out=sq, in_=b, func=AF.Square), r=[("xt", ch % 2)], w=["sq"])
                s.op("dve", lambda e: e.tensor_tensor(out=ssp, in0=ssp, in1=sq, op=ALU.add), r=["sq", "ssp"], w=["ssp"])
        for tg in range(c.TG):
            s.op("pe", lambda e, tg=tg: e.matmul(s.ps[:, tg, :], s.onesf, ssp[:, tg * 512:(tg + 1) * 512], start=True, stop=True),
                 r=["ssp", "onesf"], w=[("ps", tg)])
        pa, pk = s.psr(0, 128, c.TG)
        s.op("dve", lambda e: e.tensor_scalar(out=rstd, in0=pa, scalar1=1.0 / nfeat, scalar2=c.EPS, op0=ALU.mult, op1=ALU.add),
             r=pk, w=["rstd"])
        s.op("act", lambda e: e.activation(out=rstd, in_=rstd, func=AF.Sqrt), r=["rstd"], w=["rstd"])
        s.op("dve", lambda e: e.reciprocal(out=rstd, in_=rstd), r=["rstd"], w=["rstd"])
        s.P.barrier()
        ar.reset(mk)

    def scale_pass(s, src, nch, gain, rstd, dst):
        c, ar = s.c, s.ar
        mk = ar.mark()
        xt = [ar.alloc([c.T], F32) for _ in range(2)]
        for ch in range(nch):
            b = xt[ch % 2]
            s.dma("sp", b, src[ch * 128:(ch + 1) * 128, :], w=[("xs", ch % 2)], grp=("xs", ch % 2))
            s.op("dve", lambda e, b=b, ch=ch: e.scalar_tensor_tensor(out=dst[:, ch, :], in0=b, scalar=gain[:, ch:ch + 1], in1=rstd,
                                                                     op0=ALU.mult, op1=ALU.mult),
                 r=[("xs", ch % 2), "rstd", "gain"], w=[("hT", ch)])
        s.P.barrier()
        ar.reset(mk)

    def load_gain(s, src):
        g = s.ar.alloc([src.shape[-1]], F32)
        s.dma("sp", g, src, w=["gain"])
        return g

    def gemm(s, w2d, KC, chunks, rhs, rkeys, evac, SW, ntg, NSL=3):
        ar = s.ar
        wv = w2d.rearrange("(kc p) n -> p kc n", p=128)
        wsl = [ar.alloc([KC, SW], BF16) for _ in range(NSL)]
        slabs, cur, used = [], [], 0
        for ch in chunks:
            if used + ch[1] > SW:
                slabs.append(cur)
                cur, used = [], 0
            cur.append((ch, used))
            used += ch[1]
        if cur:
            slabs.append(cur)
        for si, slab in enumerate(slabs):
            slot = s.slabctr % NSL
            s.slabctr += 1
            rngs = []
            for ch, off in slab:
                if rngs and rngs[-1][0] + rngs[-1][1] == ch[0] and rngs[-1][2] + rngs[-1][1] == off:
                    rngs[-1][1] += ch[1]
                else:
                    rngs.append([ch[0], ch[1], off])
            for c0, n, off in rngs:
                s.dma("pool", wsl[slot][:, :, off:off + n], wv[:, :, c0:c0 + n], w=[("wsl", slot)], grp=("wsl", slot), nowaw=(off > 0))
            for ch, off in slab:
                c0, m, tag, k0, k1 = ch
                half = s.halfctr % 2
                s.halfctr += 1
                for kc in range(k0, k1):
                    for tg in range(ntg):
                        s.op("pe", lambda e, o=s.ps[0:m, half * 4 + tg, :], lw=wsl[slot][:, kc, off:off + m], rr=rhs(kc, tg),
                             s0=(kc == k0), s1=(kc == k1 - 1): e.matmul(o, lw, rr, start=s0, stop=s1),
                             r=[("wsl", slot)] + rkeys(kc), w=[("ps", half * 4 + tg)])
                pa, pk = s.psr(half, m, ntg)
                evac(tag, pa, pk, m, half)

    def mk_store(s, name, nslots, Tn, dt, func):
        stg = [s.ar.alloc([Tn], dt) for _ in range(nslots)]
        st = {"i": 0}

        def ev(dst, pa, pk, m, half):
            slot = st["i"] % nslots
            st["i"] += 1
            sb = stg[slot][0:m]
            s.op("act", lambda e: e.activation(out=sb, in_=pa, func=func), r=pk, w=[(name, slot)])
            s.dma("sp", dst, sb, r=[(name, slot)], grp=(name, slot))
        return ev

    def dump(s, l, sq, names):
        if not getattr(s.c, "DEBUG", False) or l != 0 or sq != 0:
            return
        s.P.barrier()
        for nm in names:
            src = s.sc[nm]
            s.uid += 1
            t = s.nc.dram_tensor("dbg_%s" % nm, list(src.shape), src.dtype, kind="ExternalOutput").ap()
            s.dma("sp", t, src)

    def layer(s, sq, l, xin, xmid, xout):
        s.ph_in(l, xin)
        s.dump(l, sq, ["cq", "ckv", "kr", "qswa", "kswa", "vswa", "qfox", "kfox", "vfox", "z", "gates"])
        s.ph_mla_prep(l)
        s.dump(l, sq, ["qn", "qr", "kn", "vm", "krr"])
        s.ph_attn(l)
        s.dump(l, sq, ["o"])
        s.ph_merge(l)
        s.dump(l, sq, ["merged"])
        s.ph_wout(l)
        s.dump(l, sq, ["m"])
        s.ph_resid(xin, s.inp["g_mix_post"][l], xmid)
        s.dump(l, sq, ["xa"])
        s.ph_ffn1(l, xmid)
        s.dump(l, sq, ["a"])
        s.ph_ffn2(l)
        s.ph_resid(xmid, s.inp["g_ffn_post"][l], xout)

    def norm_to_sbuf(s, src, nch, nfeat, gain_src, dst):
        ar = s.ar
        g = s.load_gain(gain_src)
        rstd = ar.alloc([s.c.T], F32)
        s.sumsq_rstd(src, nch, nfeat, rstd)
        s.scale_pass(src, nch, g, rstd, dst)

    def ph_in(s, l, xin):
        c, ar, P, sc = s.c, s.ar, s.P, s.sc
        T, KC, TG = c.T, c.KC, c.TG
        P.barrier()
        ar.reset(s.base)
        hT = ar.alloc([KC, T], BF16)
        mk = ar.mark()
        s.norm_to_sbuf(xin, KC, c.D, s.inp["g_mix_pre"][l], hT)
        P.barrier()
        ar.reset(mk)
        dsts = [(sc["cq"], "f"), (sc["ckv"], "f"), (sc["kr"], "f"), (sc["qswa"], "b"), (sc["kswa"], "b"), (sc["vswa"], "b"),
                (sc["qfox"], "b"), (sc["kfox"], "b"), (sc["vfox"], "b"), (sc["z"], "f"), (sc["gates"], "s")]
        chunks, col = [], 0
        for (dst, kind), n in zip(dsts, c.splits):
            r0 = 0
            while r0 < n:
                m = min(128, n - r0)
                chunks.append((col + r0, m, (dst[r0:r0 + m, :], kind), 0, KC))
                r0 += m
            col += n
        evf = s.mk_store("stf", 2, T, F32, AF.Copy)
        evb = s.mk_store("stb", 2, T, BF16, AF.Copy)
        evs = s.mk_store("sts", 2, T, BF16, AF.Sigmoid)

        def evac(tag, pa, pk, m, half):
            dst, kind = tag
            {"f": evf, "b": evb, "s": evs}[kind](dst, pa, pk, m, half)
        s.gemm(s.inp["w_in"][l], KC, chunks, lambda kc, tg: hT[:, kc, tg * 512:(tg + 1) * 512],
               lambda kc: [("hT", kc)], evac, 128, TG)

    def rope_store(s, name, xs, xkeys, dst, half):
        c, ar = s.c, s.ar
        T, TG = c.T, c.TG
        t1 = s.rope_t1
        ob = s.rope_ob
        for tg in range(TG):
            s.op("pe", lambda e, tg=tg: e.matmul(s.ps[0:64, half * 4 + tg, :], s.RT, xs[:, tg * 512:(tg + 1) * 512], start=True, stop=True),
                 r=list(xkeys) + ["cf"], w=[("ps", half * 4 + tg)])
        pa, pk = s.psr(half, 64, TG)
        s.op("dve", lambda e: e.tensor_tensor(out=t1, in0=xs, in1=s.cosT, op=ALU.mult), r=list(xkeys) + ["cosT"], w=["rt1"])
        s.op("dve", lambda e: e.tensor_tensor(out=s.rope_t2, in0=pa, in1=s.sinT, op=ALU.mult), r=pk + ["sinT"], w=["rt2"])
        s.op("dve", lambda e: e.tensor_tensor(out=ob, in0=t1, in1=s.rope_t2, op=ALU.add), r=["rt1", "rt2"], w=["rob"])
        s.dma("sp", dst, ob, r=["rob"], grp="rob")

    def ph_mla_prep(s, l):
        c, ar, P, sc, i = s.c, s.ar, s.P, s.sc, s.inp
        T, TG = c.T, c.TG
        QC, VC = c.QL // 128, c.KVL // 128
        P.barrier()
        ar.reset(s.base)
        cqn = ar.alloc([QC, T], BF16)
        ckvn = ar.alloc([VC, T], BF16)
        s.rope_t1 = ar.alloc([T], F32, parts=64)
        s.rope_t2 = ar.alloc([T], F32, parts=64)
        s.rope_ob = ar.alloc([T], BF16, parts=64)
        xs = ar.alloc([T], F32, parts=64)
        s.cosT = ar.alloc([T], F32, parts=64)
        s.sinT = ar.alloc([T], F32, parts=64)
        s.dma("sp", s.cosT, sc["cos"], w=["cosT"])
        s.dma("sp", s.sinT, sc["sin"], w=["sinT"])
        mk = ar.mark()
        s.norm_to_sbuf(sc["cq"], QC, c.QL, i["g_q_lora"][l], cqn)
        P.barrier()
        ar.reset(mk)
        s.norm_to_sbuf(sc["ckv"], VC, c.KVL, i["g_kv_lora"][l], ckvn)
        s.dma("sp", xs, sc["kr"], w=["xs"], grp="xsld")
        s.rope_store("kr", xs, ["xs"], sc["krr"], 1)
        P.barrier()
        ar.reset(mk)
        evb = s.mk_store("stb", 2, T, BF16, AF.Copy)
        chunks = []
        for h in range(c.MLA_H):
            chunks.append((h * 192, 128, ("n", sc["qn"][h * 128:(h + 1) * 128, :]), 0, QC))
            chunks.append((h * 192 + 128, 64, ("r", sc["qr"][h * 64:(h + 1) * 64, :]), 0, QC))

        def evq(tag, pa, pk, m, half):
            kind, dst = tag
            if kind == "n":
                evb(dst, pa, pk, m, half)
            else:
                s.op("act", lambda e: e.activation(out=xs, in_=pa, func=AF.Copy), r=pk, w=["xs"])
                s.rope_store("qr", xs, ["xs"], dst, half)
        s.gemm(i["w_uq"][l], QC, chunks, lambda kc, tg: cqn[:, kc, tg * 512:(tg + 1) * 512],
               lambda kc: [("hT", kc)], evq, 192, TG)
        P.barrier()
        chunks = []
        for h in range(c.MLA_H):
            chunks.append((h * 256, 128, sc["kn"][h * 128:(h + 1) * 128, :], 0, VC))
            chunks.append((h * 256 + 128, 128, sc["vm"][h * 128:(h + 1) * 128, :], 0, VC))
        s.gemm(i["w_ukv"][l], VC, chunks, lambda kc, tg: ckvn[:, kc, tg * 512:(tg + 1) * 512],
               lambda kc: [("hT", kc)], evb, 256, TG)

    def ph_attn(s, l):
        c, ar, P, sc, i = s.c, s.ar, s.P, s.sc, s.inp
        T, NT, FH, SH = c.T, c.NT, c.FOX_H, c.SWA_H
        P.barrier()
        ar.reset(s.base)
        zT = ar.alloc([T], F32, parts=FH)
        bfor = ar.alloc([FH], F32)
        zt = ar.alloc([NT, FH], F32)
        Fp = ar.alloc([NT, FH], F32)
        off = ar.alloc([NT, FH], F32)
        Fref = ar.alloc([NT, FH], F32)
        biasF = ar.alloc([FH, NT, NT], F32)
        s.dma("sp", zT, sc["z"], w=["zT"])
        s.dma("sp", bfor, i["b_forget"][l], w=["bfor"])
        pz = s.ps[:, 7, 0:NT * FH].rearrange("p (a b) -> p a b", b=FH)
        for j in range(NT):
            s.op("pe", lambda e, j=j: e.transpose(out=s.ps[:, 7, j * FH:(j + 1) * FH], in_=zT[0:FH, j * 128:(j + 1) * 128],
                                                  identity=s.identf[0:FH, 0:FH]), r=["zT", "cf"], w=[("ps", 7)])
        for j in range(NT):
            s.op("dve", lambda e, j=j: e.tensor_tensor(out=zt[:, j, :], in0=pz[:, j, :], in1=bfor, op=ALU.add),
                 r=[("ps", 7), "bfor"], w=["zt"])
        ztf = zt.rearrange("p a b -> p (a b)")
        s.op("act", lambda e: e.activation(out=ztf, in_=ztf, func=AF.Exp, scale=-1.0), r=["zt"], w=["zt"])
        s.op("act", lambda e: e.activation(out=ztf, in_=ztf, func=AF.Ln, bias=s.onesf[:, 0:1], scale=1.0), r=["zt", "onesf"], w=["zt"])
        n = NT * FH
        s.op("pe", lambda e: e.matmul(s.ps[:, 6, 0:n], s.U, ztf, start=True, stop=True), r=["zt", "cf"], w=[("ps", 6)])
        s.op("pe", lambda e: e.matmul(s.ps[:, 5, 0:n], s.onesf, ztf, start=True, stop=True), r=["zt", "onesf"], w=[("ps", 5)])
        pt = s.ps[:, 5, 0:n].rearrange("p (a b) -> p a b", b=FH)
        s.op("dve", lambda e: e.memset(off[:, 0, :], 0.0), w=["off"])
        for j in range(1, NT):
            s.op("dve", lambda e, j=j: e.tensor_tensor(out=off[:, j, :], in0=off[:, j - 1, :], in1=pt[:, j - 1, :], op=ALU.add),
                 r=[("ps", 5), "off"], w=["off"])
        s.op("dve", lambda e: e.tensor_tensor(out=Fp.rearrange("p a b -> p (a b)"), in0=s.ps[:, 6, 0:n],
                                              in1=off.rearrange("p a b -> p (a b)"), op=ALU.add), r=[("ps", 6), "off"], w=["Fp"])
        s.op("pe", lambda e: e.matmul(s.ps[:, 7, 0:n], s.E64, Fp.rearrange("p a b -> p (a b)"), start=True, stop=True),
             r=["Fp", "cf"], w=[("ps", 7)])
        s.op("dve", lambda e: e.tensor_copy(out=Fref.rearrange("p a b -> p (a b)"), in_=s.ps[:, 7, 0:n]), r=[("ps", 7)], w=["Fref"])
        for h in range(FH):
            for kt in range(NT):
                s.op("dve", lambda e, h=h, kt=kt: e.tensor_scalar(out=biasF[:, h, kt, :], in0=Fref[:, :, h], scalar1=Fp[:, kt, h:h + 1],
                                                                  scalar2=-1.0, op0=ALU.subtract, op1=ALU.mult),
                     r=["Fref", "Fp"], w=["biasF"])
        slopes = [2.0 ** (-8.0 * (h + 1) / SH) for h in range(SH)]
        biasS = ar.alloc([SH, NT, 2], F32)
        for h in range(SH):
            s.op("dve", lambda e, h=h: e.tensor_scalar(out=biasS[:, h, :, 0], in0=s.dcur, scalar1=slopes[h], scalar2=None, op0=ALU.mult),
                 r=["dcur"], w=["biasS"])
            s.op("dve", lambda e, h=h: e.tensor_scalar(out=biasS[:, h, :, 1], in0=s.dprev, scalar1=slopes[h], scalar2=None, op0=ALU.mult),
                 r=["dprev"], w=["biasS"])
        sinks = ar.alloc([1], F32, parts=16)
        s.dma("sp", sinks, i["swa_sinks"][l], w=["sinks"])

        sinkT = ar.alloc([T], BF16, parts=16)
        s.op("act", lambda e: e.activation(out=sinkT[0:SH], in_=s.posrel[0:SH], func=AF.Exp, bias=sinks[0:SH], scale=s.slopes[0:SH]),
             r=["posrel", "sinks", "cf"], w=["sinkT"])
        s.sinkT = sinkT
        s.dbg("sinkT", sinkT[0:SH], [SH, T], BF16)
        s.dbg("biasS", biasS.rearrange("p a b c -> p (a b c)"), [128, SH * NT * 2])
        s.dbg("biasF", biasF.rearrange("p a b c -> p (a b c)"), [128, FH * NT * NT])
        s.dbg("Fp", Fp.rearrange("p a b -> p (a b)"), [128, NT * FH])
        s.dbg("zt", ztf, [128, NT * FH])
        NB = 2
        s.hb = []
        for b in range(NB):
            s.hb.append(dict(q=ar.alloc([T], BF16), k=ar.alloc([T], BF16), v=ar.alloc([T], BF16), vt=ar.alloc([NT, 128], BF16),
                             q2=ar.alloc([T], BF16, parts=64)))
        s.k2 = ar.alloc([T], BF16, parts=64)
        s.PT = [ar.alloc([128], BF16) for _ in range(3)]
        s.rinv = ar.alloc([512], F32)
        s.ob = [ar.alloc([512], BF16) for _ in range(2)]
        s.hctr = 0
        s.sctr = 0
        s.pctr = 0
        s.gctr = 0
        s.dma("sp", s.k2, sc["krr"], w=["k2"])
        causal = lambda qb: [(kt, "tri" if kt == qb else None) for kt in range(qb + 1)]
        banded = lambda qb: ([(qb - 1, "low")] if qb > 0 else []) + [(qb, "tri")]
        o = sc["o"]
        for h in range(c.MLA_H):
            s.attn_head(sc["qn"][h * 128:(h + 1) * 128], sc["kn"][h * 128:(h + 1) * 128], sc["vm"][h * 128:(h + 1) * 128],
                        sc["qr"][h * 64:(h + 1) * 64], 192.0 ** -0.5, lambda kt, qb, rel: 0.0, causal, None, o[h * 128:(h + 1) * 128])
        G = SH // c.SWA_KV
        for h in range(SH):
            kv = h // G
            s.attn_head(sc["qswa"][h * 128:(h + 1) * 128], sc["kswa"][kv * 128:(kv + 1) * 128], sc["vswa"][kv * 128:(kv + 1) * 128],
                        None, 128.0 ** -0.5, lambda kt, qb, rel, h=h: biasS[:, h, qb, rel:rel + 1], banded, h,
                        o[c.MLA_OUT + h * 128:c.MLA_OUT + (h + 1) * 128])
        for h in range(FH):
            r0 = c.MLA_OUT + c.SWA_OUT + h * 128
            s.attn_head(sc["qfox"][h * 128:(h + 1) * 128], sc["kfox"][h * 128:(h + 1) * 128], sc["vfox"][h * 128:(h + 1) * 128],
                        None, 128.0 ** -0.5, lambda kt, qb, rel, h=h: biasF[:, h, kt, qb:qb + 1], causal, None, o[r0:r0 + 128])

    def attn_head(s, qsrc, ksrc, vsrc, q2src, scale, bias_fn, keytiles, sink, dst):
        c = s.c
        T, NT = c.T, c.NT
        b = s.hctr % 2
        s.hctr += 1
        hb = s.hb[b]
        kq, kk, kv_, kvt, kq2 = ("hq", b), ("hk", b), ("hv", b), ("hvt", b), ("hq2", b)
        s.dma("sp", hb["q"], qsrc, w=[kq], grp=kq)
        s.dma("sp", hb["k"], ksrc, w=[kk], grp=kk)
        s.dma("sp", hb["v"], vsrc, w=[kv_], grp=kv_)
        if q2src is not None:
            s.dma("sp", hb["q2"][0:64], q2src, w=[kq2], grp=kq2)
        for j0 in range(0, NT, 8):
            nb = min(8, NT - j0)
            bank = 6 + (j0 // 8) % 2
            pv = s.ps[:, bank, :].bitcast(BF16)
            for j in range(nb):
                s.op("pe", lambda e, j=j, j0=j0, pv=pv: e.transpose(out=pv[:, j * 128:(j + 1) * 128], in_=hb["v"][:, (j0 + j) * 128:(j0 + j + 1) * 128],
                                                                    identity=s.identb), r=[kv_, "identb"], w=[("ps", bank)])
            s.op("act", lambda e, j0=j0, nb=nb, pv=pv: e.activation(out=hb["vt"][:, j0:j0 + nb, :].rearrange("p a b -> p (a b)"),
                                                                    in_=pv[:, 0:nb * 128], func=AF.Copy), r=[("ps", bank)], w=[kvt])
        tiles = []
        for qb in range(NT):
            kl = keytiles(qb)
            for ti, (kt, mt) in enumerate(kl):
                tiles.append((qb, kt, mt, ti == 0, ti == len(kl) - 1))

        def emit_S(t):
            qb, kt, mt, first, last = t
            sb = s.sctr % 2
            s.sctr += 1
            o_ = s.ps[:, sb, 0:128]
            s.op("pe", lambda e: e.matmul(o_, hb["k"][:, kt * 128:(kt + 1) * 128], hb["q"][:, qb * 128:(qb + 1) * 128],
                                          start=True, stop=(q2src is None)), r=[kk, kq], w=[("ps", sb)])
            if q2src is not None:
                s.op("pe", lambda e: e.matmul(o_, s.k2[0:64, kt * 128:(kt + 1) * 128], hb["q2"][0:64, qb * 128:(qb + 1) * 128],
                                              start=False, stop=True), r=["k2", kq2], w=[("ps", sb)])
            return sb

        def emit_PV(t, sb):
            qb, kt, mt, first, last = t
            qg, qi = qb // 4, qb % 4
            ot, rb = 2 + qg % 2, 4 + qg % 2
            cols = slice(qi * 128, (qi + 1) * 128)
            ps_ = s.pctr % 3
            s.pctr += 1
            PT = s.PT[ps_]
            rel = 0 if kt == qb else 1
            bias = bias_fn(kt, qb, rel)
            bk = [] if isinstance(bias, float) else ["biasF", "biasS"]
            s.op("act", lambda e: e.activation(out=PT, in_=s.ps[:, sb, 0:128], func=AF.Exp, bias=bias, scale=scale),
                 r=[("ps", sb)] + bk, w=[("PT", ps_)])
            if mt is not None:
                mk = s.tri if mt == "tri" else s.low
                s.op("pool", lambda e: e.tensor_tensor(out=PT, in0=PT, in1=mk, op=ALU.mult), r=[("PT", ps_), mt], w=[("PT", ps_)])
            s.op("pe", lambda e: e.matmul(s.ps[:, ot, cols], hb["vt"][:, kt, :], PT, start=first, stop=last),
                 r=[kvt, ("PT", ps_)], w=[("ps", ot)])
            s.op("pe", lambda e: e.matmul(s.ps[:, rb, cols], s.onesb, PT, start=first, stop=(last and sink is None)),
                 r=["onesb", ("PT", ps_)], w=[("ps", rb)])
            if last and sink is not None:
                SH = c.SWA_H
                s.op("pe", lambda e: e.matmul(s.ps[:, rb, cols], s.oh[0:SH, sink, :], s.sinkT[0:SH, qb * 128:(qb + 1) * 128],
                                              start=False, stop=True), r=["oh", "sinkT"], w=[("ps", rb)])
            if last and qi == 3:
                gslot = s.gctr % 2
                s.gctr += 1
                s.op("dve", lambda e: e.reciprocal(out=s.rinv, in_=s.ps[:, rb, :]), r=[("ps", rb)], w=["rinv"])
                s.op("dve", lambda e: e.tensor_tensor(out=s.ob[gslot], in0=s.ps[:, ot, :], in1=s.rinv, op=ALU.mult),
                     r=[("ps", ot), "rinv"], w=[("ob", gslot)])
                s.dma("sp", dst[:, qg * 512:(qg + 1) * 512], s.ob[gslot], r=[("ob", gslot)], grp=("ob", gslot))

        pend = None
        for t in tiles:
            sb = emit_S(t)
            if pend is not None:
                emit_PV(*pend)
            pend = (t, sb)
        emit_PV(*pend)

    def load_fm(s, src, nch, dst):
        for ch in range(nch):
            s.dma("sp", dst[:, ch, :], src[ch * 128:(ch + 1) * 128, :], w=["hTall"], grp="ldfm", nowaw=True)

    def ph_merge(s, l):
        c, ar, P, sc, i = s.c, s.ar, s.P, s.sc, s.inp
        T, KC, TG, D = c.T, c.KC, c.TG, c.D
        P.barrier()
        ar.reset(s.base)
        oT = ar.alloc([KC, T], BF16)
        s.load_fm(sc["o"], KC, oT)
        gt = [[ar.alloc([T], BF16) for _ in range(3)]] * 2
        acc = ar.alloc([T], F32)
        tmp = ar.alloc([T], F32)
        mst = [ar.alloc([T], BF16) for _ in range(2)]
        kA, kB = c.MLA_OUT // 128, (c.MLA_OUT + c.SWA_OUT) // 128
        kr = [(0, kA), (kA, kB), (kB, KC)]
        chunks = []
        for cc in range(KC):
            for br in range(3):
                chunks.append((cc * 128, 128, (cc, br), kr[br][0], kr[br][1]))

        def evac(tag, pa, pk, m, half):
            cc, br = tag
            gs = cc % 2
            if br == 0:
                for b3 in range(3):
                    s.dma("sp", gt[0][b3], sc["gates"][b3 * D + cc * 128:b3 * D + (cc + 1) * 128, :], w=[("gt", 0, b3)], grp=("gt", 0, b3))
                s.op("dve", lambda e: e.tensor_tensor(out=acc, in0=pa, in1=gt[gs][0], op=ALU.mult), r=pk + [("gt", 0, 0)], w=["acc"])
            elif br == 1:
                s.op("dve", lambda e: e.tensor_tensor(out=tmp, in0=pa, in1=gt[gs][1], op=ALU.mult), r=pk + [("gt", 0, 1)], w=["tmp"])
                s.op("pool", lambda e: e.tensor_tensor(out=acc, in0=acc, in1=tmp, op=ALU.add), r=["acc", "tmp"], w=["acc"])
            else:
                s.op("dve", lambda e: e.tensor_tensor(out=tmp, in0=pa, in1=gt[gs][2], op=ALU.mult), r=pk + [("gt", 0, 2)], w=["tmp"])
                s.op("pool", lambda e: e.tensor_tensor(out=mst[gs], in0=acc, in1=tmp, op=ALU.add), r=["acc", "tmp"], w=[("mst", gs)])
                s.dma("sp", sc["merged"][cc * 128:(cc + 1) * 128, :], mst[gs], r=[("mst", gs)], grp=("mst", gs))
        s.gemm(i["w_branch"][l], KC, chunks, lambda kc, tg: oT[:, kc, tg * 512:(tg + 1) * 512],
               lambda kc: ["hTall"], evac, 128, TG)

    def ph_wout(s, l):
        c, ar, P, sc, i = s.c, s.ar, s.P, s.sc, s.inp
        T, KC, TG = c.T, c.KC, c.TG
        P.barrier()
        ar.reset(s.base)
        mT = ar.alloc([KC, T], BF16)
        s.load_fm(sc["merged"], KC, mT)
        evf = s.mk_store("stf", 2, T, F32, AF.Copy)
        chunks = [(cc * 128, 128, sc["m"][cc * 128:(cc + 1) * 128, :], 0, KC) for cc in range(KC)]
        s.gemm(i["w_out"][l], KC, chunks, lambda kc, tg: mT[:, kc, tg * 512:(tg + 1) * 512],
               lambda kc: ["hTall"], evf, 128, TG)

    def ph_resid(s, xsrc, gain_src, xdst):
        c, ar, P, sc = s.c, s.ar, s.P, s.sc
        T, KC = c.T, c.KC
        P.barrier()
        ar.reset(s.base)
        g = s.load_gain(gain_src)
        rstd = ar.alloc([T], F32)
        s.sumsq_rstd(sc["m"], KC, c.D, rstd)
        mt = [ar.alloc([T], F32) for _ in range(2)]
        xt = [ar.alloc([T], F32) for _ in range(2)]
        for ch in range(KC):
            b = ch % 2
            rows = slice(ch * 128, (ch + 1) * 128)
            s.dma("sp", mt[b], sc["m"][rows, :], w=[("rm", b)], grp=("rm", b))
            s.dma("sp", xt[b], xsrc[rows, :], w=[("rx", b)], grp=("rx", b))
            s.op("dve", lambda e, b=b, ch=ch: e.scalar_tensor_tensor(out=mt[b], in0=mt[b], scalar=g[:, ch:ch + 1], in1=rstd,
                                                                     op0=ALU.mult, op1=ALU.mult), r=[("rm", b), "rstd", "gain"], w=[("rm", b)])
            s.op("pool", lambda e, b=b: e.tensor_tensor(out=xt[b], in0=xt[b], in1=mt[b], op=ALU.add), r=[("rm", b), ("rx", b)], w=[("rx", b)])
            s.dma("sp", xdst[rows, :], xt[b], r=[("rx", b)], grp=("rxs", b))

    def ph_ffn1(s, l, xmid):
        c, ar, P, sc, i = s.c, s.ar, s.P, s.sc, s.inp
        T, KC, TG = c.T, c.KC, c.TG
        P.barrier()
        ar.reset(s.base)
        hT = ar.alloc([KC, T], BF16)
        mk = ar.mark()
        s.norm_to_sbuf(xmid, KC, c.D, i["g_ffn_pre"][l], hT)
        P.barrier()
        ar.reset(mk)
        sg = [ar.alloc([T], F32) for _ in range(2)]
        ast = [ar.alloc([T], BF16) for _ in range(2)]
        NJ = c.FFN // 128
        chunks = []
        for j in range(NJ):
            chunks.append((j * 128, 128, (j, 0), 0, KC))
            chunks.append((c.FFN + j * 128, 128, (j, 1), 0, KC))

        def evac(tag, pa, pk, m, half):
            j, u = tag
            b = j % 2
            if u == 0:
                s.op("act", lambda e: e.activation(out=sg[b], in_=pa, func=AF.Silu), r=pk, w=[("sg", b)])
            else:
                s.op("dve", lambda e: e.tensor_tensor(out=ast[b], in0=pa, in1=sg[b], op=ALU.mult), r=pk + [("sg", b)], w=[("ast", b)])
                s.dma("sp", sc["a"][j * 128:(j + 1) * 128, :], ast[b], r=[("ast", b)], grp=("ast", b))
        s.gemm(i["w_gate_up"][l], KC, chunks, lambda kc, tg: hT[:, kc, tg * 512:(tg + 1) * 512],
               lambda kc: [("hT", kc)], evac, 128, TG, NSL=4)

    def ph_ffn2(s, l):
        c, ar, P, sc, i = s.c, s.ar, s.P, s.sc, s.inp
        T, KC, TG = c.T, c.KC, c.TG
        NJ = c.FFN // 128
        for tq in range(TG):
            P.barrier()
            ar.reset(s.base)
            aT = ar.alloc([NJ, 512], BF16)
            for j in range(NJ):
                s.dma("sp", aT[:, j, :], sc["a"][j * 128:(j + 1) * 128, tq * 512:(tq + 1) * 512], w=["hTall"], grp="ldfm", nowaw=True)
            evf = s.mk_store("stf", 3, 512, F32, AF.Copy)
            chunks = [(cc * 128, 128, sc["m"][cc * 128:(cc + 1) * 128, tq * 512:(tq + 1) * 512], 0, NJ) for cc in range(KC)]
            s.gemm(i["w_down"][l], NJ, chunks, lambda kc, tg: aT[:, kc, :], lambda kc: ["hTall"], evf, 128, 1)


def make_consts(cfg):
    cf = np.zeros((128, 578), np.float32)
    cf[:, 0:128] = np.eye(128, dtype=np.float32)
    cf[:, 128:256] = np.triu(np.ones((128, 128), np.float32))
    cf[64, 256:384] = 1.0
    cf[:, 384:512] = np.triu(np.ones((128, 128), np.float32))
    RT = np.zeros((64, 64), np.float32)
    for m in range(32):
        RT[m + 32, m] = -1.0
        RT[m, m + 32] = 1.0
    cf[0:64, 512:576] = RT
    half = 32
    inv = (10000.0 ** (-np.arange(half, dtype=np.float32) / half)).astype(np.float32)
    cf[0:64, 576] = np.concatenate([inv, inv])
    SH = cfg.SWA_H
    cf[0:SH, 577] = (2.0 ** (-8.0 * np.arange(1, SH + 1, dtype=np.float32) / SH)).astype(np.float32)
    oh = np.zeros((16, 16, 128), np.float32)
    for h in range(16):
        oh[h, h, :] = 1.0
    return cf, oh.reshape(16, 16 * 128)


def make_in_map(cfg, seqs, inputs):
    c = cfg
    T, NT = c.T, c.NT
    x = inputs["x"]
    pos = np.asarray(inputs["positions"]).astype(np.int32)
    m = {}
    m["xT"] = np.ascontiguousarray(np.stack([x[b].T for b in seqs]))
    p = np.stack([pos[b] for b in seqs])
    m["posb"] = np.ascontiguousarray(np.broadcast_to(p[:, None, :], (len(seqs), 64, T)))
    pk = p.reshape(len(seqs), NT, 128).transpose(0, 2, 1)
    m["posk"] = np.ascontiguousarray(pk)
    pr = p.reshape(len(seqs), NT, 128)[:, :, 64]
    m["posr"] = np.ascontiguousarray(np.broadcast_to(pr[:, None, :], (len(seqs), 128, NT)))
    m["posrow"] = np.ascontiguousarray(np.broadcast_to(p[:, None, :], (len(seqs), 16, T)))
    prr = np.repeat(pr, 128, axis=1)
    m["posrrow"] = np.ascontiguousarray(np.broadcast_to(prr[:, None, :], (len(seqs), 16, T)))

    def fm(g, n):
        return np.ascontiguousarray(g.reshape(g.shape[0], n // 128, 128).transpose(0, 2, 1))
    for g in ("g_mix_pre", "g_mix_post", "g_ffn_pre", "g_ffn_post"):
        m[g] = fm(inputs[g], c.D)
    m["g_q_lora"] = fm(inputs["g_q_lora"], c.QL)
    m["g_kv_lora"] = fm(inputs["g_kv_lora"], c.KVL)
    bf = inputs["b_forget"]
    m["b_forget"] = np.ascontiguousarray(np.broadcast_to(bf[:, None, :], (bf.shape[0], 128, bf.shape[1])))
    sk = np.zeros((inputs["swa_sinks"].shape[0], 16, 1), np.float32)
    sk[:, 0:c.SWA_H, 0] = inputs["swa_sinks"]
    m["swa_sinks"] = sk
    for w in ("w_in", "w_uq", "w_ukv", "w_branch", "w_out", "w_gate_up", "w_down"):
        m[w] = inputs[w]
    cf, oh = make_consts(c)
    m["cst_f"] = cf
    m["cst_oh"] = oh
    return m


NCORES = 8
_CACHE = {}


def run(cfg, ncores, inputs, trace=False):
    B = inputs["x"].shape[0]
    assert B == ncores * cfg.NSEQ
    key = (cfg.D, cfg.T, cfg.DEPTH, cfg.NSEQ)
    if key not in _CACHE:
        _CACHE[key] = Builder(cfg).build()
    nc = _CACHE[key]
    in_maps = [make_in_map(cfg, list(range(cid * cfg.NSEQ, (cid + 1) * cfg.NSEQ)), inputs) for cid in range(ncores)]
    res = run_bass_kernel_spmd(nc, in_maps, core_ids=list(range(ncores)), **({"trace": True} if trace else {}))
    outs = []
    for cid in range(ncores):
        o = res.results[cid]["out"]
        for j in range(cfg.NSEQ):
            outs.append(o[j].T)
    return np.ascontiguousarray(np.stack(outs)).astype(np.float32), res


def kernel(**inputs):
    inputs = {k: np.asarray(v) for k, v in inputs.items()}
    cfg = Cfg(D=4096, T=2048, DEPTH=2, NSEQ=8 // NCORES)
    out, _ = run(cfg, NCORES, inputs)
    return out
```

```python
import contextlib
import os
import numpy as np
import concourse.bass as bass
import concourse.mybir as mybir
from concourse.bass_utils import run_bass_kernel_spmd

F32 = mybir.dt.float32
BF16 = mybir.dt.bfloat16
I32 = mybir.dt.int32
ALU = mybir.AluOpType
AF = mybir.ActivationFunctionType
AX = mybir.AxisListType

SAME_ENGINE_SYNC = True
Q2 = ("sp", "act") if os.environ.get("K_Q2", "1") == "1" else ("sp", "sp")


class Op:
    __slots__ = ("eng", "fn", "deps", "dma", "signal", "count", "dcount", "idx", "inc")


class Prog:
    STREAMS = ("pe", "act", "dve", "pool", "sp")

    def __init__(self):
        self.ops = []
        self.lastw = {}
        self.rd_eng = {}
        self.rd_dma = {}
        self.bar = set()
        self.bar_pending = set()
        self.since_dma = []
        self.last_eng = {}

    def barrier(self):
        self.bar = set(self.last_eng.values()) | set(self.since_dma)
        self.since_dma = []
        self.bar_pending = set(self.STREAMS)

    def add(self, eng, fn, r=(), w=(), dma=None, inc=16, nowaw=False):
        i = len(self.ops)
        deps = set()
        lw = self.lastw
        for k in r:
            j = lw.get(k)
            if j is not None:
                deps.add(j)
        for k in w:
            j = lw.get(k)
            if j is not None and not nowaw:
                deps.add(j)
            d = self.rd_eng.get(k)
            if d:
                deps.update(d.values())
            d = self.rd_dma.get(k)
            if d:
                deps.update(d)
        if eng in self.bar_pending:
            deps |= self.bar
            self.bar_pending.discard(eng)
        if dma is None:
            self.last_eng[eng] = i
        else:
            self.since_dma.append(i)
        for k in r:
            if dma is None:
                self.rd_eng.setdefault(k, {})[eng] = i
            else:
                self.rd_dma.setdefault(k, []).append(i)
        for k in w:
            lw[k] = i
            self.rd_eng[k] = {}
            self.rd_dma[k] = []
        op = Op()
        op.eng, op.fn, op.deps, op.dma, op.signal, op.idx = eng, fn, deps, dma, False, i
        op.count = 0
        op.inc = inc
        op.dcount = 0
        self.ops.append(op)
        return i

    def emit(self, nc):
        ops = self.ops
        for op in ops:
            for j in op.deps:
                d = ops[j]
                if d.dma is not None:
                    continue
                if d.eng == op.eng and op.dma is None and (op.eng == "pe" or not SAME_ENGINE_SYNC):
                    continue
                d.signal = True
        cnt = {s: 0 for s in self.STREAMS}
        dcnt = {}
        for op in ops:
            if op.dma is not None:
                dcnt[op.dma] = dcnt.get(op.dma, 0) + op.inc
                op.dcount = dcnt[op.dma]
            elif op.signal:
                cnt[op.eng] += 1
                op.count = cnt[op.eng]
        self.stats = dict(cnt=cnt, ndma_groups=len(dcnt), nops=len(ops))
        with contextlib.ExitStack() as st:
            esem = {s: st.enter_context(nc.semaphore("e_" + s)) for s in self.STREAMS}
            dsem = {g: st.enter_context(nc.semaphore("d_%d" % n)) for n, g in enumerate(dcnt)}
            block = st.enter_context(nc.Block())

            def run(name, eng):
                waited = {}
                for op in ops:
                    if op.eng != name:
                        continue
                    need = {}
                    for j in op.deps:
                        d = ops[j]
                        if d.dma is not None:
                            s, v = dsem[d.dma], d.dcount
                        elif d.eng == op.eng and op.dma is None and (op.eng == "pe" or not SAME_ENGINE_SYNC):
                            continue
                        else:
                            s, v = esem[d.eng], d.count
                        key = id(s)
                        if need.get(key, (None, 0))[1] < v:
                            need[key] = (s, v)
                    for key, (s, v) in need.items():
                        if waited.get(key, 0) < v:
                            eng.wait_ge(s, v)
                            waited[key] = v
                    ins = op.fn(eng)
                    if op.dma is not None:
                        ins.then_inc(dsem[op.dma], op.inc)
                    elif op.signal:
                        ins.then_inc(esem[op.eng], 1)
                if name == "sp":
                    for g, v in dcnt.items():
                        eng.wait_ge(dsem[g], v)

            block.tensor(lambda e: run("pe", e))
            block.scalar(lambda e: run("act", e))
            block.vector(lambda e: run("dve", e))
            block.gpsimd(lambda e: run("pool", e))
            block.sync(lambda e: run("sp", e))


class Arena:
    def __init__(self, nc, st, nbytes, name="arena"):
        self.t = st.enter_context(nc.sbuf_tensor(name, [128, nbytes // 2], BF16))
        self.nbytes = nbytes
        self.off = 0
        self.marks = []

    def alloc(self, shape_free, dtype, parts=128):
        esz = 4 if dtype in (F32, I32) else 2
        n = int(np.prod(shape_free))
        nb = (n * esz + 31) // 32 * 32
        assert self.off + nb <= self.nbytes, ("SBUF arena overflow", self.off, nb, self.nbytes)
        a = self.t[0:parts, self.off // 2:(self.off + nb) // 2]
        self.off += nb
        if esz == 4:
            a = a.bitcast(dtype)
        a = a[:, 0:n]
        if len(shape_free) == 2:
            a = a.rearrange("p (a b) -> p a b", b=shape_free[1])
        elif len(shape_free) == 3:
            a = a.rearrange("p (a b c) -> p a b c", b=shape_free[1], c=shape_free[2])
        return a

    def mark(self):
        return self.off

    def reset(self, m):
        self.off = m


class Cfg:
    def __init__(s, D=4096, T=2048, DEPTH=2, NSEQ=1):
        s.D, s.T, s.DEPTH, s.NSEQ = D, T, DEPTH, NSEQ
        NH = D // 128
        s.MLA_H, s.QL, s.KVL, s.ROPE = NH // 4, D // 4, D // 8, 64
        s.SWA_H = NH // 2
        s.SWA_KV = max(1, s.SWA_H // 8)
        s.FOX_H = NH // 4
        s.FFN = ((8 * D // 3 + 255) // 256) * 256
        s.splits = [s.QL, s.KVL, 64, s.SWA_H * 128, s.SWA_KV * 128, s.SWA_KV * 128,
                    s.FOX_H * 128, s.FOX_H * 128, s.FOX_H * 128, s.FOX_H, 3 * D]
        s.INW = sum(s.splits)
        s.KC, s.NT, s.TG = D // 128, T // 128, T // 512
        s.MLA_OUT, s.SWA_OUT, s.FOX_OUT = s.MLA_H * 128, s.SWA_H * 128, s.FOX_H * 128
        s.EPS = 1e-6


PI = float(np.pi)


class Builder:
    def __init__(s, cfg):
        s.c = cfg
        s.nc = bass.Bass("TRN2", target_bir_lowering=False)
        s.P = Prog()
        s.uid = 0
        s.slabctr = 0
        s.halfctr = 0

    def din(s, name, shape, dt=F32):
        return s.nc.dram_tensor(name, list(shape), dt, kind="ExternalInput").ap()

    def dscr(s, name, shape, dt):
        return s.nc.dram_tensor(name, list(shape), dt, kind="Internal").ap()

    def dma(s, q, out, in_, r=(), w=(), grp=None, nowaw=False, **kw):
        if grp is None:
            grp = "misc"
            r = list(r) + ["miscchain"]
            w = list(w) + ["miscchain"]
        s.P.add(q, lambda e: e.dma_start(out=out, in_=in_, **kw), r=r, w=w, dma=grp, nowaw=nowaw)

    def dbg(s, name, ap, shape, dt=F32):
        if not getattr(s.c, "DEBUG", False):
            return
        s.uid += 1
        t = s.nc.dram_tensor("dbg_%s_%d" % (name, s.uid), list(shape), dt, kind="ExternalOutput").ap()
        s.dma("sp", t, ap, r=[name])

    def op(s, eng, fn, r=(), w=()):
        s.P.add(eng, fn, r=r, w=w)

    def build(s):
        c, nc = s.c, s.nc
        D, T, L, NS = c.D, c.T, c.DEPTH, c.NSEQ
        KC, NT, TG = c.KC, c.NT, c.TG
        i = s.inp = {}
        i["xT"] = s.din("xT", [NS, D, T])
        i["posb"] = s.din("posb", [NS, 64, T], I32)
        i["posk"] = s.din("posk", [NS, 128, NT], I32)
        i["posr"] = s.din("posr", [NS, 128, NT], I32)
        i["posrow"] = s.din("posrow", [NS, 16, T], I32)
        i["posrrow"] = s.din("posrrow", [NS, 16, T], I32)
        for g in ("g_mix_pre", "g_mix_post", "g_ffn_pre", "g_ffn_post"):
            i[g] = s.din(g, [L, 128, KC])
        i["g_q_lora"] = s.din("g_q_lora", [L, 128, c.QL // 128])
        i["g_kv_lora"] = s.din("g_kv_lora", [L, 128, c.KVL // 128])
        i["b_forget"] = s.din("b_forget", [L, 128, c.FOX_H])
        i["swa_sinks"] = s.din("swa_sinks", [L, 16, 1])
        i["w_in"] = s.din("w_in", [L, D, c.INW])
        i["w_uq"] = s.din("w_uq", [L, c.QL, c.MLA_H * 192])
        i["w_ukv"] = s.din("w_ukv", [L, c.KVL, c.MLA_H * 256])
        i["w_branch"] = s.din("w_branch", [L, D, D])
        i["w_out"] = s.din("w_out", [L, D, D])
        i["w_gate_up"] = s.din("w_gate_up", [L, D, 2 * c.FFN])
        i["w_down"] = s.din("w_down", [L, c.FFN, D])
        i["cst_f"] = s.din("cst_f", [128, 128 * 4 + 64 + 2])
        i["cst_oh"] = s.din("cst_oh", [16, 16 * 128])
        s.out = nc.dram_tensor("out", [NS, D, T], F32, kind="ExternalOutput").ap()
        sc = s.sc = {}
        sc["cq"] = s.dscr("sc_cq", [c.QL, T], F32)
        sc["ckv"] = s.dscr("sc_ckv", [c.KVL, T], F32)
        sc["kr"] = s.dscr("sc_kr", [64, T], F32)
        sc["z"] = s.dscr("sc_z", [c.FOX_H, T], F32)
        for nm, rows in (("qswa", c.SWA_H * 128), ("kswa", c.SWA_KV * 128), ("vswa", c.SWA_KV * 128),
                         ("qfox", c.FOX_H * 128), ("kfox", c.FOX_H * 128), ("vfox", c.FOX_H * 128),
                         ("gates", 3 * D), ("qn", c.MLA_H * 128), ("qr", c.MLA_H * 64), ("kn", c.MLA_H * 128),
                         ("vm", c.MLA_H * 128), ("krr", 64), ("o", D), ("merged", D), ("a", c.FFN)):
            sc[nm] = s.dscr("sc_" + nm, [rows, T], BF16)
        sc["m"] = s.dscr("sc_m", [D, T], F32)
        sc["wdb"] = s.dscr("sc_wdb", [KC // 2, 128, c.FFN // 128, 256], BF16)
        sc["xa"] = s.dscr("sc_xa", [D, T], F32)
        sc["xb"] = s.dscr("sc_xb", [D, T], F32)
        with contextlib.ExitStack() as st:
            s.ar = Arena(nc, st, 206 * 1024)
            s.ps = st.enter_context(nc.psum_tensor("ps", [128, 8, 512], F32))
            s.consts()
            for sq in range(NS):
                s.seq_prep(sq)
                for l in range(L):
                    xin = i["xT"][sq] if l == 0 else sc["xb"]
                    xout = s.out[sq] if l == L - 1 else sc["xb"]
                    s.layer(sq, l, xin, sc["xa"], xout)
            s.P.emit(nc)
        return nc

    def consts(s):
        ar, P, i = s.ar, s.P, s.inp
        cf = ar.alloc([128 * 4 + 66], F32)
        s.dma("sp", cf, i["cst_f"], w=["cf"])
        s.identf = cf[:, 0:128]
        s.U = cf[:, 128:256]
        s.E64 = cf[:, 256:384]
        s.RT = cf[0:64, 512:576]
        s.invf = cf[0:64, 576:577]
        s.slopes = cf[0:16, 577:578]
        s.identb = ar.alloc([128], BF16)
        s.tri = ar.alloc([128], BF16)
        s.low = ar.alloc([128], BF16)
        s.onesb = ar.alloc([128], BF16)
        s.onesf = ar.alloc([128], F32)
        s.op("dve", lambda e: e.tensor_copy(out=s.identb, in_=s.identf), r=["cf"], w=["identb"])
        s.op("dve", lambda e: e.tensor_copy(out=s.tri, in_=cf[:, 384:512]), r=["cf"], w=["tri"])
        s.op("dve", lambda e: e.tensor_scalar(out=s.low, in0=cf[:, 384:512], scalar1=-1.0, scalar2=1.0,
                                              op0=ALU.mult, op1=ALU.add), r=["cf"], w=["low"])
        s.op("dve", lambda e: e.memset(s.onesb, 1.0), w=["onesb"])
        s.op("dve", lambda e: e.memset(s.onesf, 1.0), w=["onesf"])
        s.oh = ar.alloc([16, 128], BF16, parts=16)
        T, NT = s.c.T, s.c.NT
        s.dcur = ar.alloc([NT], F32)
        s.dprev = ar.alloc([NT], F32)
        s.posrel = ar.alloc([T], F32, parts=16)
        s.base = ar.mark()
        ohf = ar.alloc([16 * 128], F32, parts=16)
        s.dma("sp", ohf, i["cst_oh"], w=["ohf"])
        s.op("dve", lambda e: e.tensor_copy(out=s.oh.rearrange("p a b -> p (a b)"), in_=ohf), r=["ohf"], w=["oh"])
        s.sc["cos"] = s.dscr("sc_cos", [64, T], F32)
        s.sc["sin"] = s.dscr("sc_sin", [64, T], F32)
        s.sc["posrel"] = s.dscr("sc_posrel", [16, T], F32)

    def seq_prep(s, sq):
        c, ar, P, i = s.c, s.ar, s.P, s.inp
        T, NT = c.T, c.NT
        P.barrier()
        ar.reset(s.base)
        pi_ = ar.alloc([T], I32, parts=64)
        pf = ar.alloc([T], F32, parts=64)
        ang = ar.alloc([T], F32, parts=64)
        tmp = ar.alloc([T], F32, parts=64)
        s.dma("sp", pi_, i["posb"][sq], w=["pi"])
        s.op("dve", lambda e: e.tensor_copy(out=pf, in_=pi_), r=["pi"], w=["pf"])
        s.op("dve", lambda e: e.tensor_scalar(out=ang, in0=pf, scalar1=s.invf, scalar2=None, op0=ALU.mult),
             r=["pf", "cf"], w=["ang"])
        MAGIC = 12582912.0
        u = ar.alloc([T], F32, parts=64)
        t2 = ar.alloc([T], F32, parts=64)
        s.sinT = ar.alloc([T], F32, parts=64)
        s.cosT = ar.alloc([T], F32, parts=64)
        for tab, sh, nm in ((s.sinT, 0.0, "sinT"), (s.cosT, 0.25, "cosT")):
            s.op("dve", lambda e, sh=sh: e.tensor_scalar(out=u, in0=ang, scalar1=1.0 / (2 * PI), scalar2=sh,
                                                         op0=ALU.mult, op1=ALU.add), r=["ang"], w=["u"])
            s.op("dve", lambda e: e.tensor_scalar(out=tmp, in0=u, scalar1=MAGIC, scalar2=None, op0=ALU.add), r=["u"], w=["tmp"])
            s.op("dve", lambda e: e.tensor_scalar(out=t2, in0=tmp, scalar1=MAGIC, scalar2=None, op0=ALU.subtract), r=["tmp"], w=["t2"])
            s.op("dve", lambda e: e.tensor_tensor(out=u, in0=u, in1=t2, op=ALU.subtract), r=["u", "t2"], w=["u"])
            s.op("act", lambda e, tab=tab: e.activation(out=tab, in_=u, func=AF.Sin, scale=2 * PI), r=["u"], w=[nm])
            s.dbg(nm, tab, [64, T])
            s.dma("sp", s.sc["sin" if nm == "sinT" else "cos"], tab, r=[nm])
        pk = ar.alloc([NT], I32)
        pr = ar.alloc([NT], I32)
        pkf = ar.alloc([NT], F32)
        prf = ar.alloc([NT], F32)
        s.dma("sp", pk, i["posk"][sq], w=["pk"])
        s.dma("sp", pr, i["posr"][sq], w=["pr"])
        s.op("dve", lambda e: e.tensor_copy(out=pkf, in_=pk), r=["pk"], w=["pkf"])
        s.op("dve", lambda e: e.tensor_copy(out=prf, in_=pr), r=["pr"], w=["prf"])
        s.op("dve", lambda e: e.tensor_tensor(out=s.dcur, in0=pkf, in1=prf, op=ALU.subtract), r=["pkf", "prf"], w=["dcur"])
        s.op("dve", lambda e: e.memset(s.dprev[:, 0:1], 0.0), w=["dprev"])
        if NT > 1:
            s.op("dve", lambda e: e.tensor_tensor(out=s.dprev[:, 1:NT], in0=pkf[:, 0:NT - 1], in1=prf[:, 1:NT],
                                                  op=ALU.subtract), r=["pkf", "prf"], w=["dprev"])
        a1 = ar.alloc([T], I32, parts=16)
        a2 = ar.alloc([T], I32, parts=16)
        a1f = ar.alloc([T], F32, parts=16)
        a2f = ar.alloc([T], F32, parts=16)
        s.dma("sp", a1, i["posrow"][sq], w=["a1"])
        s.dma("sp", a2, i["posrrow"][sq], w=["a2"])
        s.op("dve", lambda e: e.tensor_copy(out=a1f, in_=a1), r=["a1"], w=["a1f"])
        s.op("dve", lambda e: e.tensor_copy(out=a2f, in_=a2), r=["a2"], w=["a2f"])
        s.op("dve", lambda e: e.tensor_tensor(out=s.posrel, in0=a1f, in1=a2f, op=ALU.subtract), r=["a1f", "a2f"], w=["posrel"])

    def psr(s, half, m, ntg):
        ap = s.ps[0:m, half * 4:half * 4 + ntg, :].rearrange("p a b -> p (a b)")
        return ap, [("ps", half * 4 + t) for t in range(ntg)]

    def sumsq_rstd(s, src, nch, nfeat, rstd, Tn=None, t0=0):
        c, ar = s.c, s.ar
        T = c.T
        mk = ar.mark()
        xt = [ar.alloc([T], F32) for _ in range(2)]
        sq = ar.alloc([T], F32)
        ssp = ar.alloc([T], F32)
        for ch in range(nch):
            b = xt[ch % 2]
            s.dma(Q2[ch % 2], b, src[ch * 128:(ch + 1) * 128, :], w=[("xt", ch % 2)], grp=("xt", ch % 2))
            if ch == 0:
                s.op("act", lambda e, b=b: e.activation(out=ssp, in_=b, func=AF.Square), r=[("xt", 0)], w=["ssp"])
            else:
                s.op("act", lambda e, b=b: e.activation(out=sq, in_=b, func=AF.Square), r=[("xt", ch % 2)], w=["sq"])
                s.op("dve", lambda e: e.tensor_tensor(out=ssp, in0=ssp, in1=sq, op=ALU.add), r=["sq", "ssp"], w=["ssp"])
        for tg in range(c.TG):
            s.op("pe", lambda e, tg=tg: e.matmul(s.ps[:, tg, :], s.onesf, ssp[:, tg * 512:(tg + 1) * 512], start=True, stop=True),
                 r=["ssp", "onesf"], w=[("ps", tg)])
        pa, pk = s.psr(0, 128, c.TG)
        s.op("dve", lambda e: e.tensor_scalar(out=rstd, in0=pa, scalar1=1.0 / nfeat, scalar2=c.EPS, op0=ALU.mult, op1=ALU.add),
             r=pk, w=["rstd"])
        s.op("act", lambda e: e.activation(out=rstd, in_=rstd, func=AF.Sqrt), r=["rstd"], w=["rstd"])
        s.op("dve", lambda e: e.reciprocal(out=rstd, in_=rstd), r=["rstd"], w=["rstd"])
        s.P.barrier()
        ar.reset(mk)

    def scale_pass(s, src, nch, gain, rstd, dst):
        c, ar = s.c, s.ar
        mk = ar.mark()
        xt = [ar.alloc([c.T], F32) for _ in range(2)]
        for ch in range(nch):
            b = xt[ch % 2]
            s.dma(Q2[ch % 2], b, src[ch * 128:(ch + 1) * 128, :], w=[("xs", ch % 2)], grp=("xs", ch % 2))
            s.op("dve", lambda e, b=b, ch=ch: e.scalar_tensor_tensor(out=dst[:, ch, :], in0=b, scalar=gain[:, ch:ch + 1], in1=rstd,
                                                                     op0=ALU.mult, op1=ALU.mult),
                 r=[("xs", ch % 2), "rstd", "gain"], w=[("hT", ch)])
        s.P.barrier()
        ar.reset(mk)

    def load_gain(s, src):
        g = s.ar.alloc([src.shape[-1]], F32)
        s.dma("sp", g, src, w=["gain"])
        return g

    def gemm(s, w2d, KC, chunks, rhs, rkeys, evac, SW, ntg, NSL=3, slab_src=None, hook=None):
        ar = s.ar
        wv = w2d.rearrange("(kc p) n -> p kc n", p=128) if w2d is not None else None
        wsl = [ar.alloc([KC, SW], BF16) for _ in range(NSL)]
        slabs, cur, used = [], [], 0
        for ch in chunks:
            if used + ch[1] > SW:
                slabs.append(cur)
                cur, used = [], 0
            cur.append((ch, used))
            used += ch[1]
        if cur:
            slabs.append(cur)
        for si, slab in enumerate(slabs):
            slot = s.slabctr % NSL
            s.slabctr += 1
            rngs = []
            for ch, off in slab:
                if rngs and rngs[-1][0] + rngs[-1][1] == ch[0] and rngs[-1][2] + rngs[-1][1] == off:
                    rngs[-1][1] += ch[1]
                else:
                    rngs.append([ch[0], ch[1], off])
            if hook is not None:
                hook(si)
            if slab_src is not None:
                s.dma("sp", wsl[slot], slab_src(si), w=[("wsl", slot)], grp=("wsl", slot))
                rngs = []
            kmin = min(ch[3] for ch, _ in slab)
            kmax = max(ch[4] for ch, _ in slab)
            if os.environ.get("K_KR", "1") != "1":
                kmin, kmax = 0, KC
            for c0, n, off in rngs:
                s.dma("pool", wsl[slot][:, kmin:kmax, off:off + n], wv[:, kmin:kmax, c0:c0 + n], w=[("wsl", slot)], grp=("wsl", slot),
                      nowaw=(off > 0))
            for ch, off in slab:
                c0, m, tag, k0, k1 = ch
                half = s.halfctr % 2
                s.halfctr += 1
                for kc in range(k0, k1):
                    for tg in range(ntg):
                        s.op("pe", lambda e, o=s.ps[0:m, half * 4 + tg, :], lw=wsl[slot][:, kc, off:off + m], rr=rhs(kc, tg),
                             s0=(kc == k0), s1=(kc == k1 - 1): e.matmul(o, lw, rr, start=s0, stop=s1),
                             r=[("wsl", slot)] + rkeys(kc), w=[("ps", half * 4 + tg)])
                pa, pk = s.psr(half, m, ntg)
                evac(tag, pa, pk, m, half)

    def mk_store(s, name, nslots, Tn, dt, func, q="sp"):
        stg = [s.ar.alloc([Tn], dt) for _ in range(nslots)]
        st = {"i": 0}

        def ev(dst, pa, pk, m, half):
            slot = st["i"] % nslots
            st["i"] += 1
            sb = stg[slot][0:m]
            s.op("act", lambda e: e.activation(out=sb, in_=pa, func=func), r=pk, w=[(name, slot)])
            s.dma(q, dst, sb, r=[(name, slot)], grp=(name, slot))
        return ev

    def dump(s, l, sq, names):
        if not getattr(s.c, "DEBUG", False) or l != 0 or sq != 0:
            return
        s.P.barrier()
        for nm in names:
            src = s.sc[nm]
            s.uid += 1
            t = s.nc.dram_tensor("dbg_%s" % nm, list(src.shape), src.dtype, kind="ExternalOutput").ap()
            s.dma("sp", t, src)

    def layer(s, sq, l, xin, xmid, xout):
        s.ph_in(l, xin)
        s.dump(l, sq, ["cq", "ckv", "kr", "qswa", "kswa", "vswa", "qfox", "kfox", "vfox", "z", "gates"])
        s.ph_mla_prep(l)
        s.dump(l, sq, ["qn", "qr", "kn", "vm", "krr"])
        s.ph_attn(l)
        s.dump(l, sq, ["o"])
        s.ph_merge(l)
        s.dump(l, sq, ["merged"])
        s.ph_wout(l)
        s.dump(l, sq, ["m"])
        s.ph_resid(xin, s.inp["g_mix_post"][l], xmid)
        s.dump(l, sq, ["xa"])
        s.ph_ffn1(l, xmid)
        s.dump(l, sq, ["a"])
        s.ph_ffn2(l)
        s.ph_resid(xmid, s.inp["g_ffn_post"][l], xout)

    def norm_to_sbuf(s, src, nch, nfeat, gain_src, dst):
        ar = s.ar
        g = s.load_gain(gain_src)
        rstd = ar.alloc([s.c.T], F32)
        s.sumsq_rstd(src, nch, nfeat, rstd)
        s.scale_pass(src, nch, g, rstd, dst)

    def ph_in(s, l, xin):
        c, ar, P, sc = s.c, s.ar, s.P, s.sc
        T, KC, TG = c.T, c.KC, c.TG
        P.barrier()
        ar.reset(s.base)
        hT = ar.alloc([KC, T], BF16)
        mk = ar.mark()
        s.norm_to_sbuf(xin, KC, c.D, s.inp["g_mix_pre"][l], hT)
        P.barrier()
        ar.reset(mk)
        dsts = [(sc["cq"], "f"), (sc["ckv"], "f"), (sc["kr"], "f"), (sc["qswa"], "b"), (sc["kswa"], "b"), (sc["vswa"], "b"),
                (sc["qfox"], "b"), (sc["kfox"], "b"), (sc["vfox"], "b"), (sc["z"], "f"), (sc["gates"], "s")]
        chunks, col = [], 0
        for (dst, kind), n in zip(dsts, c.splits):
            r0 = 0
            while r0 < n:
                m = min(128, n - r0)
                chunks.append((col + r0, m, (dst[r0:r0 + m, :], kind), 0, KC))
                r0 += m
            col += n
        evf = s.mk_store("stf", 2, T, F32, AF.Copy)
        evb = s.mk_store("stb", 2, T, BF16, AF.Copy)
        evs = s.mk_store("sts", 2, T, BF16, AF.Sigmoid)

        def evac(tag, pa, pk, m, half):
            dst, kind = tag
            {"f": evf, "b": evb, "s": evs}[kind](dst, pa, pk, m, half)
        s.gemm(s.inp["w_in"][l], KC, chunks, lambda kc, tg: hT[:, kc, tg * 512:(tg + 1) * 512],
               lambda kc: [("hT", kc)], evac, 128, TG)

    def rope_store(s, name, xs, xkeys, dst, half):
        c, ar = s.c, s.ar
        T, TG = c.T, c.TG
        t1 = s.rope_t1
        ob = s.rope_ob
        for tg in range(TG):
            s.op("pe", lambda e, tg=tg: e.matmul(s.ps[0:64, half * 4 + tg, :], s.RT, xs[:, tg * 512:(tg + 1) * 512], start=True, stop=True),
                 r=list(xkeys) + ["cf"], w=[("ps", half * 4 + tg)])
        pa, pk = s.psr(half, 64, TG)
        s.op("dve", lambda e: e.tensor_tensor(out=t1, in0=xs, in1=s.cosT, op=ALU.mult), r=list(xkeys) + ["cosT"], w=["rt1"])
        s.op("dve", lambda e: e.tensor_tensor(out=s.rope_t2, in0=pa, in1=s.sinT, op=ALU.mult), r=pk + ["sinT"], w=["rt2"])
        s.op("dve", lambda e: e.tensor_tensor(out=ob, in0=t1, in1=s.rope_t2, op=ALU.add), r=["rt1", "rt2"], w=["rob"])
        s.dma("sp", dst, ob, r=["rob"], grp="rob")

    def ph_mla_prep(s, l):
        c, ar, P, sc, i = s.c, s.ar, s.P, s.sc, s.inp
        T, TG = c.T, c.TG
        QC, VC = c.QL // 128, c.KVL // 128
        P.barrier()
        ar.reset(s.base)
        cqn = ar.alloc([QC, T], BF16)
        ckvn = ar.alloc([VC, T], BF16)
        s.rope_t1 = ar.alloc([T], F32, parts=64)
        s.rope_t2 = ar.alloc([T], F32, parts=64)
        s.rope_ob = ar.alloc([T], BF16, parts=64)
        xs = ar.alloc([T], F32, parts=64)
        s.cosT = ar.alloc([T], F32, parts=64)
        s.sinT = ar.alloc([T], F32, parts=64)
        s.dma("sp", s.cosT, sc["cos"], w=["cosT"])
        s.dma("sp", s.sinT, sc["sin"], w=["sinT"])
        mk = ar.mark()
        s.norm_to_sbuf(sc["cq"], QC, c.QL, i["g_q_lora"][l], cqn)
        P.barrier()
        ar.reset(mk)
        s.norm_to_sbuf(sc["ckv"], VC, c.KVL, i["g_kv_lora"][l], ckvn)
        s.dma("sp", xs, sc["kr"], w=["xs"], grp="xsld")
        s.rope_store("kr", xs, ["xs"], sc["krr"], 1)
        P.barrier()
        ar.reset(mk)
        evb = s.mk_store("stb", 2, T, BF16, AF.Copy)
        chunks = []
        for h in range(c.MLA_H):
            chunks.append((h * 192, 128, ("n", sc["qn"][h * 128:(h + 1) * 128, :]), 0, QC))
            chunks.append((h * 192 + 128, 64, ("r", sc["qr"][h * 64:(h + 1) * 64, :]), 0, QC))

        def evq(tag, pa, pk, m, half):
            kind, dst = tag
            if kind == "n":
                evb(dst, pa, pk, m, half)
            else:
                s.op("act", lambda e: e.activation(out=xs, in_=pa, func=AF.Copy), r=pk, w=["xs"])
                s.rope_store("qr", xs, ["xs"], dst, half)
        s.gemm(i["w_uq"][l], QC, chunks, lambda kc, tg: cqn[:, kc, tg * 512:(tg + 1) * 512],
               lambda kc: [("hT", kc)], evq, 192, TG)
        P.barrier()
        chunks = []
        for h in range(c.MLA_H):
            chunks.append((h * 256, 128, sc["kn"][h * 128:(h + 1) * 128, :], 0, VC))
            chunks.append((h * 256 + 128, 128, sc["vm"][h * 128:(h + 1) * 128, :], 0, VC))
        s.gemm(i["w_ukv"][l], VC, chunks, lambda kc, tg: ckvn[:, kc, tg * 512:(tg + 1) * 512],
               lambda kc: [("hT", kc)], evb, 256, TG)

    def ph_attn(s, l):
        c, ar, P, sc, i = s.c, s.ar, s.P, s.sc, s.inp
        T, NT, FH, SH = c.T, c.NT, c.FOX_H, c.SWA_H
        P.barrier()
        ar.reset(s.base)
        zT = ar.alloc([T], F32, parts=FH)
        bfor = ar.alloc([FH], F32)
        zt = ar.alloc([NT, FH], F32)
        Fp = ar.alloc([NT, FH], F32)
        off = ar.alloc([NT, FH], F32)
        Fref = ar.alloc([NT, FH], F32)
        biasF = ar.alloc([FH, NT, NT], F32)
        s.dma("sp", zT, sc["z"], w=["zT"])
        s.dma("sp", bfor, i["b_forget"][l], w=["bfor"])
        pz = s.ps[:, 7, 0:NT * FH].rearrange("p (a b) -> p a b", b=FH)
        for j in range(NT):
            s.op("pe", lambda e, j=j: e.transpose(out=s.ps[:, 7, j * FH:(j + 1) * FH], in_=zT[0:FH, j * 128:(j + 1) * 128],
                                                  identity=s.identf[0:FH, 0:FH]), r=["zT", "cf"], w=[("ps", 7)])
        for j in range(NT):
            s.op("dve", lambda e, j=j: e.tensor_tensor(out=zt[:, j, :], in0=pz[:, j, :], in1=bfor, op=ALU.add),
                 r=[("ps", 7), "bfor"], w=["zt"])
        ztf = zt.rearrange("p a b -> p (a b)")
        s.op("act", lambda e: e.activation(out=ztf, in_=ztf, func=AF.Exp, scale=-1.0), r=["zt"], w=["zt"])
        s.op("act", lambda e: e.activation(out=ztf, in_=ztf, func=AF.Ln, bias=s.onesf[:, 0:1], scale=1.0), r=["zt", "onesf"], w=["zt"])
        n = NT * FH
        s.op("pe", lambda e: e.matmul(s.ps[:, 6, 0:n], s.U, ztf, start=True, stop=True), r=["zt", "cf"], w=[("ps", 6)])
        s.op("pe", lambda e: e.matmul(s.ps[:, 5, 0:n], s.onesf, ztf, start=True, stop=True), r=["zt", "onesf"], w=[("ps", 5)])
        pt = s.ps[:, 5, 0:n].rearrange("p (a b) -> p a b", b=FH)
        s.op("dve", lambda e: e.memset(off[:, 0, :], 0.0), w=["off"])
        for j in range(1, NT):
            s.op("dve", lambda e, j=j: e.tensor_tensor(out=off[:, j, :], in0=off[:, j - 1, :], in1=pt[:, j - 1, :], op=ALU.add),
                 r=[("ps", 5), "off"], w=["off"])
        s.op("dve", lambda e: e.tensor_tensor(out=Fp.rearrange("p a b -> p (a b)"), in0=s.ps[:, 6, 0:n],
                                              in1=off.rearrange("p a b -> p (a b)"), op=ALU.add), r=[("ps", 6), "off"], w=["Fp"])
        s.op("pe", lambda e: e.matmul(s.ps[:, 7, 0:n], s.E64, Fp.rearrange("p a b -> p (a b)"), start=True, stop=True),
             r=["Fp", "cf"], w=[("ps", 7)])
        s.op("dve", lambda e: e.tensor_copy(out=Fref.rearrange("p a b -> p (a b)"), in_=s.ps[:, 7, 0:n]), r=[("ps", 7)], w=["Fref"])
        for h in range(FH):
            for kt in range(NT):
                s.op("dve", lambda e, h=h, kt=kt: e.tensor_scalar(out=biasF[:, h, kt, :], in0=Fref[:, :, h], scalar1=Fp[:, kt, h:h + 1],
                                                                  scalar2=-1.0, op0=ALU.subtract, op1=ALU.mult),
                     r=["Fref", "Fp"], w=["biasF"])
        slopes = [2.0 ** (-8.0 * (h + 1) / SH) for h in range(SH)]
        biasS = ar.alloc([SH, NT, 2], F32)
        for h in range(SH):
            s.op("dve", lambda e, h=h: e.tensor_scalar(out=biasS[:, h, :, 0], in0=s.dcur, scalar1=slopes[h], scalar2=None, op0=ALU.mult),
                 r=["dcur"], w=["biasS"])
            s.op("dve", lambda e, h=h: e.tensor_scalar(out=biasS[:, h, :, 1], in0=s.dprev, scalar1=slopes[h], scalar2=None, op0=ALU.mult),
                 r=["dprev"], w=["biasS"])
        sinks = ar.alloc([1], F32, parts=16)
        s.dma("sp", sinks, i["swa_sinks"][l], w=["sinks"])

        sinkT = ar.alloc([T], BF16, parts=16)
        s.op("act", lambda e: e.activation(out=sinkT[0:SH], in_=s.posrel[0:SH], func=AF.Exp, bias=sinks[0:SH], scale=s.slopes[0:SH]),
             r=["posrel", "sinks", "cf"], w=["sinkT"])
        s.sinkT = sinkT
        s.dbg("sinkT", sinkT[0:SH], [SH, T], BF16)
        s.dbg("biasS", biasS.rearrange("p a b c -> p (a b c)"), [128, SH * NT * 2])
        s.dbg("biasF", biasF.rearrange("p a b c -> p (a b c)"), [128, FH * NT * NT])
        s.dbg("Fp", Fp.rearrange("p a b -> p (a b)"), [128, NT * FH])
        s.dbg("zt", ztf, [128, NT * FH])
        NB = 2
        s.hb = []
        for b in range(NB):
            s.hb.append(dict(q=ar.alloc([T], BF16), k=ar.alloc([T], BF16), v=ar.alloc([T], BF16), vt=ar.alloc([NT, 128], BF16),
                             q2=ar.alloc([T], BF16, parts=64)))
        s.k2 = ar.alloc([T], BF16, parts=64)
        s.PT = [ar.alloc([128], BF16) for _ in range(4)]
        s.rinv = ar.alloc([512], F32)
        s.ob = [ar.alloc([512], BF16) for _ in range(2)]
        s.hctr = 0
        s.sctr = 0
        s.pctr = 0
        s.gctr = 0
        s.dma("sp", s.k2, sc["krr"], w=["k2"])
        causal = lambda qb: [(kt, "tri" if kt == qb else None) for kt in range(qb + 1)]
        banded = lambda qb: ([(qb - 1, "low")] if qb > 0 else []) + [(qb, "tri")]
        o = sc["o"]
        heads = []
        for h in range(c.MLA_H):
            heads.append((sc["qn"][h * 128:(h + 1) * 128], sc["kn"][h * 128:(h + 1) * 128], sc["vm"][h * 128:(h + 1) * 128],
                          sc["qr"][h * 64:(h + 1) * 64], 192.0 ** -0.5, lambda kt, qb, rel: 0.0, causal, None, o[h * 128:(h + 1) * 128]))
        G = SH // c.SWA_KV
        for h in range(SH):
            kv = h // G
            heads.append((sc["qswa"][h * 128:(h + 1) * 128], sc["kswa"][kv * 128:(kv + 1) * 128], sc["vswa"][kv * 128:(kv + 1) * 128],
                          None, 128.0 ** -0.5, lambda kt, qb, rel, h=h: biasS[:, h, qb, rel:rel + 1], banded, h,
                          o[c.MLA_OUT + h * 128:c.MLA_OUT + (h + 1) * 128]))
        for h in range(FH):
            r0 = c.MLA_OUT + c.SWA_OUT + h * 128
            heads.append((sc["qfox"][h * 128:(h + 1) * 128], sc["kfox"][h * 128:(h + 1) * 128], sc["vfox"][h * 128:(h + 1) * 128],
                          None, 128.0 ** -0.5, lambda kt, qb, rel, h=h: biasF[:, h, kt, qb:qb + 1], causal, None, o[r0:r0 + 128]))
        s.attn_load(heads[0], 0)
        for hi, hd in enumerate(heads):
            if hi + 1 < len(heads):
                s.attn_load(heads[hi + 1], (hi + 1) % 2)
            s.attn_head(hi % 2, *hd)

    def attn_load(s, hd, b):
        qsrc, ksrc, vsrc, q2src = hd[0:4]
        hb = s.hb[b]
        kq, kk, kv_, kq2 = ("hq", b), ("hk", b), ("hv", b), ("hq2", b)
        s.dma("sp", hb["q"], qsrc, w=[kq], grp=kq)
        s.dma("sp", hb["k"], ksrc, w=[kk], grp=kk)
        s.dma("sp", hb["v"], vsrc, w=[kv_], grp=kv_)
        if q2src is not None:
            s.dma("sp", hb["q2"][0:64], q2src, w=[kq2], grp=kq2)

    def attn_head(s, b, qsrc, ksrc, vsrc, q2src, scale, bias_fn, keytiles, sink, dst):
        c = s.c
        T, NT = c.T, c.NT
        hb = s.hb[b]
        kq, kk, kv_, kvt, kq2 = ("hq", b), ("hk", b), ("hv", b), ("hvt", b), ("hq2", b)
        SBANK = (0, 1, 2, 3)
        for j0 in range(0, NT, 8):
            nb = min(8, NT - j0)
            bank = (j0 // 8) % 2
            pv = s.ps[:, bank, :].bitcast(BF16)
            for j in range(nb):
                s.op("pe", lambda e, j=j, j0=j0, pv=pv: e.transpose(out=pv[:, j * 128:(j + 1) * 128], in_=hb["v"][:, (j0 + j) * 128:(j0 + j + 1) * 128],
                                                                    identity=s.identb), r=[kv_, "identb"], w=[("ps", bank)])
            s.op("act", lambda e, j0=j0, nb=nb, pv=pv: e.activation(out=hb["vt"][:, j0:j0 + nb, :].rearrange("p a b -> p (a b)"),
                                                                    in_=pv[:, 0:nb * 128], func=AF.Copy), r=[("ps", bank)], w=[kvt])
        tiles = []
        for qb in range(NT):
            kl = keytiles(qb)
            for ti, (kt, mt) in enumerate(kl):
                tiles.append((qb, kt, mt, ti == 0, ti == len(kl) - 1))

        def emit_S(t):
            qb, kt, mt, first, last = t
            sb = s.sctr % 4
            s.sctr += 1
            o_ = s.ps[:, SBANK[sb], 0:128]
            s.op("pe", lambda e: e.matmul(o_, hb["k"][:, kt * 128:(kt + 1) * 128], hb["q"][:, qb * 128:(qb + 1) * 128],
                                          start=True, stop=(q2src is None)), r=[kk, kq], w=[("ps", SBANK[sb])])
            if q2src is not None:
                s.op("pe", lambda e: e.matmul(o_, s.k2[0:64, kt * 128:(kt + 1) * 128], hb["q2"][0:64, qb * 128:(qb + 1) * 128],
                                              start=False, stop=True), r=["k2", kq2], w=[("ps", SBANK[sb])])
            return sb

        def emit_PV(t, sb):
            qb, kt, mt, first, last = t
            qg, qi = qb // 4, qb % 4
            ot, rb = 4 + qg % 2, 6 + qg % 2
            cols = slice(qi * 128, (qi + 1) * 128)
            rcols = cols
            ps_ = s.pctr % 4
            s.pctr += 1
            sin_ = s.ps[:, SBANK[sb], 0:128]
            PT = s.PT[ps_]
            rel = 0 if kt == qb else 1
            bias = bias_fn(kt, qb, rel)
            bk = [] if isinstance(bias, float) else ["biasF", "biasS"]
            s.op("act", lambda e: e.activation(out=PT, in_=sin_, func=AF.Exp, bias=bias, scale=scale),
                 r=[("ps", SBANK[sb])] + bk, w=[("PT", ps_)])
            if mt is not None:
                mk = s.tri if mt == "tri" else s.low
                s.op("pool", lambda e: e.tensor_tensor(out=PT, in0=PT, in1=mk, op=ALU.mult), r=[("PT", ps_), mt], w=[("PT", ps_)])
            s.op("pe", lambda e: e.matmul(s.ps[:, ot, cols], hb["vt"][:, kt, :], PT, start=first, stop=last),
                 r=[kvt, ("PT", ps_)], w=[("ps", ot)])
            s.op("pe", lambda e: e.matmul(s.ps[:, rb, rcols], s.onesb, PT, start=first, stop=(last and sink is None)),
                 r=["onesb", ("PT", ps_)], w=[("ps", rb)])
            if last and sink is not None:
                SH = c.SWA_H
                s.op("pe", lambda e: e.matmul(s.ps[:, rb, rcols], s.oh[0:SH, sink, :], s.sinkT[0:SH, qb * 128:(qb + 1) * 128],
                                              start=False, stop=True), r=["oh", "sinkT"], w=[("ps", rb)])
            if last and qi == 3:
                gslot = s.gctr % 2
                s.gctr += 1
                s.op("dve", lambda e: e.reciprocal(out=s.rinv, in_=s.ps[:, rb, :]), r=[("ps", rb)], w=["rinv"])
                s.op("dve", lambda e: e.tensor_tensor(out=s.ob[gslot], in0=s.ps[:, ot, :], in1=s.rinv, op=ALU.mult),
                     r=[("ps", ot), "rinv"], w=[("ob", gslot)])
                s.dma("sp", dst[:, qg * 512:(qg + 1) * 512], s.ob[gslot], r=[("ob", gslot)], grp=("ob", gslot))

        pend = []
        for t in tiles:
            pend.append((t, emit_S(t)))
            if len(pend) > int(os.environ.get("K_LA", "3")):
                emit_PV(*pend.pop(0))
        while pend:
            emit_PV(*pend.pop(0))


    def load_fm(s, src, nch, dst):
        for ch in range(nch):
            s.dma("sp", dst[:, ch, :], src[ch * 128:(ch + 1) * 128, :], w=["hTall"], grp="ldfm", nowaw=True)

    def ph_merge(s, l):
        c, ar, P, sc, i = s.c, s.ar, s.P, s.sc, s.inp
        T, KC, TG, D = c.T, c.KC, c.TG, c.D
        P.barrier()
        ar.reset(s.base)
        oT = ar.alloc([KC, T], BF16)
        s.load_fm(sc["o"], KC, oT)
        gt = [[ar.alloc([T], BF16) for _ in range(3)]] * 2
        acc = ar.alloc([T], F32)
        tmp = ar.alloc([T], F32)
        mst = [ar.alloc([T], BF16) for _ in range(2)]
        kA, kB = c.MLA_OUT // 128, (c.MLA_OUT + c.SWA_OUT) // 128
        kr = [(0, kA), (kA, kB), (kB, KC)]
        chunks = []
        for cc in range(KC):
            for br in range(3):
                chunks.append((cc * 128, 128, (cc, br), kr[br][0], kr[br][1]))

        def evac(tag, pa, pk, m, half):
            cc, br = tag
            gs = cc % 2
            if br == 0:
                for b3 in range(3):
                    s.dma("sp", gt[0][b3], sc["gates"][b3 * D + cc * 128:b3 * D + (cc + 1) * 128, :], w=[("gt", 0, b3)], grp=("gt", 0, b3))
                s.op("dve", lambda e: e.tensor_tensor(out=acc, in0=pa, in1=gt[gs][0], op=ALU.mult), r=pk + [("gt", 0, 0)], w=["acc"])
            elif br == 1:
                s.op("dve", lambda e: e.tensor_tensor(out=tmp, in0=pa, in1=gt[gs][1], op=ALU.mult), r=pk + [("gt", 0, 1)], w=["tmp"])
                s.op("pool", lambda e: e.tensor_tensor(out=acc, in0=acc, in1=tmp, op=ALU.add), r=["acc", "tmp"], w=["acc"])
            else:
                s.op("dve", lambda e: e.tensor_tensor(out=tmp, in0=pa, in1=gt[gs][2], op=ALU.mult), r=pk + [("gt", 0, 2)], w=["tmp"])
                s.op("pool", lambda e: e.tensor_tensor(out=mst[gs], in0=acc, in1=tmp, op=ALU.add), r=["acc", "tmp"], w=[("mst", gs)])
                s.dma("sp", sc["merged"][cc * 128:(cc + 1) * 128, :], mst[gs], r=[("mst", gs)], grp=("mst", gs))
        s.gemm(i["w_branch"][l], KC, chunks, lambda kc, tg: oT[:, kc, tg * 512:(tg + 1) * 512],
               lambda kc: ["hTall"], evac, 128, TG)

    def ph_wout(s, l):
        c, ar, P, sc, i = s.c, s.ar, s.P, s.sc, s.inp
        T, KC, TG = c.T, c.KC, c.TG
        P.barrier()
        ar.reset(s.base)
        mT = ar.alloc([KC, T], BF16)
        s.load_fm(sc["merged"], KC, mT)
        evf = s.mk_store("stf", 2, T, F32, AF.Copy)
        chunks = [(cc * 128, 128, sc["m"][cc * 128:(cc + 1) * 128, :], 0, KC) for cc in range(KC)]
        s.gemm(i["w_out"][l], KC, chunks, lambda kc, tg: mT[:, kc, tg * 512:(tg + 1) * 512],
               lambda kc: ["hTall"], evf, 128, TG)

    def ph_resid(s, xsrc, gain_src, xdst):
        c, ar, P, sc = s.c, s.ar, s.P, s.sc
        T, KC = c.T, c.KC
        P.barrier()
        ar.reset(s.base)
        g = s.load_gain(gain_src)
        rstd = ar.alloc([T], F32)
        s.sumsq_rstd(sc["m"], KC, c.D, rstd)
        mt = [ar.alloc([T], F32) for _ in range(2)]
        xt = [ar.alloc([T], F32) for _ in range(2)]
        for ch in range(KC):
            b = ch % 2
            rows = slice(ch * 128, (ch + 1) * 128)
            s.dma("sp", mt[b], sc["m"][rows, :], w=[("rm", b)], grp=("rm", b))
            s.dma(Q2[1], xt[b], xsrc[rows, :], w=[("rx", b)], grp=("rx", b))
            s.op("dve", lambda e, b=b, ch=ch: e.scalar_tensor_tensor(out=mt[b], in0=mt[b], scalar=g[:, ch:ch + 1], in1=rstd,
                                                                     op0=ALU.mult, op1=ALU.mult), r=[("rm", b), "rstd", "gain"], w=[("rm", b)])
            s.op("pool", lambda e, b=b: e.tensor_tensor(out=xt[b], in0=xt[b], in1=mt[b], op=ALU.add), r=[("rm", b), ("rx", b)], w=[("rx", b)])
            s.dma(Q2[b], xdst[rows, :], xt[b], r=[("rx", b)], grp=("rxs", b))

    def ph_ffn1(s, l, xmid):
        c, ar, P, sc, i = s.c, s.ar, s.P, s.sc, s.inp
        T, KC, TG = c.T, c.KC, c.TG
        P.barrier()
        ar.reset(s.base)
        hT = ar.alloc([KC, T], BF16)
        mk = ar.mark()
        s.norm_to_sbuf(xmid, KC, c.D, i["g_ffn_pre"][l], hT)
        P.barrier()
        ar.reset(mk)
        sg = [ar.alloc([T], F32) for _ in range(2)]
        ast = [ar.alloc([T], BF16) for _ in range(2)]
        NJ = c.FFN // 128
        chunks = []
        for j in range(NJ):
            chunks.append((j * 128, 128, (j, 0), 0, KC))
            chunks.append((c.FFN + j * 128, 128, (j, 1), 0, KC))

        def evac(tag, pa, pk, m, half):
            j, u = tag
            b = j % 2
            if u == 0:
                s.op("act", lambda e: e.activation(out=sg[b], in_=pa, func=AF.Silu), r=pk, w=[("sg", b)])
            else:
                s.op("dve", lambda e: e.tensor_tensor(out=ast[b], in0=pa, in1=sg[b], op=ALU.mult), r=pk + [("sg", b)], w=[("ast", b)])
                s.dma("sp", sc["a"][j * 128:(j + 1) * 128, :], ast[b], r=[("ast", b)], grp=("ast", b))
        NP = KC // 2
        wd = i["w_down"][l].rearrange("(kc p) n -> p kc n", p=128)
        nsl_tot = 2 * NJ
        step = max(1, nsl_tot // (NP + 1))

        def hook(si):
            if si % step == 0 and si // step < NP:
                pr = si // step
                s.dma("pool", sc["wdb"][pr], wd[:, :, pr * 256:(pr + 1) * 256], grp=("wdb", pr % 2))
        s.gemm(i["w_gate_up"][l], KC, chunks, lambda kc, tg: hT[:, kc, tg * 512:(tg + 1) * 512],
               lambda kc: [("hT", kc)], evac, 128, TG, NSL=4, hook=hook)

    def ph_ffn2(s, l):
        c, ar, P, sc, i = s.c, s.ar, s.P, s.sc, s.inp
        T, KC, TG = c.T, c.KC, c.TG
        NJ = c.FFN // 128
        for tq in range(TG):
            P.barrier()
            ar.reset(s.base)
            aT = ar.alloc([NJ, 512], BF16)
            for j in range(NJ):
                s.dma("sp", aT[:, j, :], sc["a"][j * 128:(j + 1) * 128, tq * 512:(tq + 1) * 512], w=["hTall"], grp="ldfm", nowaw=True)
            evf = s.mk_store("stf", 3, 512, F32, AF.Copy, q=("act" if os.environ.get("K_ACTQ", "1") == "1" else "sp"))
            chunks = [(cc * 128, 128, sc["m"][cc * 128:(cc + 1) * 128, tq * 512:(tq + 1) * 512], 0, NJ) for cc in range(KC)]
            s.gemm(None, NJ, chunks, lambda kc, tg: aT[:, kc, :], lambda kc: ["hTall"], evf, 256, 1, NSL=2,
                   slab_src=lambda si: sc["wdb"][si])


def make_consts(cfg):
    cf = np.zeros((128, 578), np.float32)
    cf[:, 0:128] = np.eye(128, dtype=np.float32)
    cf[:, 128:256] = np.triu(np.ones((128, 128), np.float32))
    cf[64, 256:384] = 1.0
    cf[:, 384:512] = np.triu(np.ones((128, 128), np.float32))
    RT = np.zeros((64, 64), np.float32)
    for m in range(32):
        RT[m + 32, m] = -1.0
        RT[m, m + 32] = 1.0
    cf[0:64, 512:576] = RT
    half = 32
    inv = (10000.0 ** (-np.arange(half, dtype=np.float32) / half)).astype(np.float32)
    cf[0:64, 576] = np.concatenate([inv, inv])
    SH = cfg.SWA_H
    cf[0:SH, 577] = (2.0 ** (-8.0 * np.arange(1, SH + 1, dtype=np.float32) / SH)).astype(np.float32)
    oh = np.zeros((16, 16, 128), np.float32)
    for h in range(16):
        oh[h, h, :] = 1.0
    return cf, oh.reshape(16, 16 * 128)


def make_in_map(cfg, seqs, inputs):
    c = cfg
    T, NT = c.T, c.NT
    x = inputs["x"]
    pos = np.asarray(inputs["positions"]).astype(np.int32)
    m = {}
    m["xT"] = np.ascontiguousarray(np.stack([x[b].T for b in seqs]))
    p = np.stack([pos[b] for b in seqs])
    m["posb"] = np.ascontiguousarray(np.broadcast_to(p[:, None, :], (len(seqs), 64, T)))
    pk = p.reshape(len(seqs), NT, 128).transpose(0, 2, 1)
    m["posk"] = np.ascontiguousarray(pk)
    pr = p.reshape(len(seqs), NT, 128)[:, :, 64]
    m["posr"] = np.ascontiguousarray(np.broadcast_to(pr[:, None, :], (len(seqs), 128, NT)))
    m["posrow"] = np.ascontiguousarray(np.broadcast_to(p[:, None, :], (len(seqs), 16, T)))
    prr = np.repeat(pr, 128, axis=1)
    m["posrrow"] = np.ascontiguousarray(np.broadcast_to(prr[:, None, :], (len(seqs), 16, T)))

    def fm(g, n):
        return np.ascontiguousarray(g.reshape(g.shape[0], n // 128, 128).transpose(0, 2, 1))
    for g in ("g_mix_pre", "g_mix_post", "g_ffn_pre", "g_ffn_post"):
        m[g] = fm(inputs[g], c.D)
    m["g_q_lora"] = fm(inputs["g_q_lora"], c.QL)
    m["g_kv_lora"] = fm(inputs["g_kv_lora"], c.KVL)
    bf = inputs["b_forget"]
    m["b_forget"] = np.ascontiguousarray(np.broadcast_to(bf[:, None, :], (bf.shape[0], 128, bf.shape[1])))
    sk = np.zeros((inputs["swa_sinks"].shape[0], 16, 1), np.float32)
    sk[:, 0:c.SWA_H, 0] = inputs["swa_sinks"]
    m["swa_sinks"] = sk
    for w in ("w_in", "w_uq", "w_ukv", "w_branch", "w_out", "w_gate_up", "w_down"):
        m[w] = inputs[w]
    cf, oh = make_consts(c)
    m["cst_f"] = cf
    m["cst_oh"] = oh
    return m


NCORES = 8
_CACHE = {}


def run(cfg, ncores, inputs, trace=False):
    B = inputs["x"].shape[0]
    assert B == ncores * cfg.NSEQ
    key = (cfg.D, cfg.T, cfg.DEPTH, cfg.NSEQ)
    if key not in _CACHE:
        _CACHE[key] = Builder(cfg).build()
    nc = _CACHE[key]
    in_maps = [make_in_map(cfg, list(range(cid * cfg.NSEQ, (cid + 1) * cfg.NSEQ)), inputs) for cid in range(ncores)]
    res = run_bass_kernel_spmd(nc, in_maps, core_ids=list(range(ncores)), **({"trace": True} if trace else {}))
    outs = []
    for cid in range(ncores):
        o = res.results[cid]["out"]
        for j in range(cfg.NSEQ):
            outs.append(o[j].T)
    return np.ascontiguousarray(np.stack(outs)).astype(np.float32), res


def kernel(**inputs):
    inputs = {k: np.asarray(v) for k, v in inputs.items()}
    cfg = Cfg(D=4096, T=2048, DEPTH=2, NSEQ=8 // NCORES)
    out, _ = run(cfg, NCORES, inputs)
    return out
```

```python
import contextlib
import os
import numpy as np
import concourse.bass as bass
import concourse.mybir as mybir
from concourse.bass_utils import run_bass_kernel_spmd

F32 = mybir.dt.float32
BF16 = mybir.dt.bfloat16
I32 = mybir.dt.int32
ALU = mybir.AluOpType
AF = mybir.ActivationFunctionType
AX = mybir.AxisListType

SAME_ENGINE_SYNC = True
Q2 = ("sp", "act") if os.environ.get("K_Q2", "1") == "1" else ("sp", "sp")


class Op:
    __slots__ = ("eng", "fn", "deps", "dma", "signal", "count", "dcount", "idx", "inc")


class Prog:
    STREAMS = ("pe", "act", "dve", "pool", "sp")

    def __init__(self):
        self.ops = []
        self.lastw = {}
        self.rd_eng = {}
        self.rd_dma = {}
        self.bar = set()
        self.bar_pending = set()
        self.since_dma = []
        self.last_eng = {}

    def barrier(self):
        self.bar = set(self.last_eng.values()) | set(self.since_dma)
        self.since_dma = []
        self.bar_pending = set(self.STREAMS)

    def add(self, eng, fn, r=(), w=(), dma=None, inc=16, nowaw=False):
        i = len(self.ops)
        deps = set()
        lw = self.lastw
        for k in r:
            j = lw.get(k)
            if j is not None:
                deps.add(j)
        for k in w:
            j = lw.get(k)
            if j is not None and not nowaw:
                deps.add(j)
            d = self.rd_eng.get(k)
            if d:
                deps.update(d.values())
            d = self.rd_dma.get(k)
            if d:
                deps.update(d)
        if eng in self.bar_pending:
            deps |= self.bar
            self.bar_pending.discard(eng)
        if dma is None:
            self.last_eng[eng] = i
        else:
            self.since_dma.append(i)
        for k in r:
            if dma is None:
                self.rd_eng.setdefault(k, {})[eng] = i
            else:
                self.rd_dma.setdefault(k, []).append(i)
        for k in w:
            lw[k] = i
            self.rd_eng[k] = {}
            self.rd_dma[k] = []
        op = Op()
        op.eng, op.fn, op.deps, op.dma, op.signal, op.idx = eng, fn, deps, dma, False, i
        op.count = 0
        op.inc = inc
        op.dcount = 0
        self.ops.append(op)
        return i

    def emit(self, nc):
        ops = self.ops
        for op in ops:
            for j in op.deps:
                d = ops[j]
                if d.dma is not None:
                    continue
                if d.eng == op.eng and op.dma is None and (op.eng == "pe" or not SAME_ENGINE_SYNC):
                    continue
                d.signal = True
        cnt = {s: 0 for s in self.STREAMS}
        dcnt = {}
        for op in ops:
            if op.dma is not None:
                dcnt[op.dma] = dcnt.get(op.dma, 0) + op.inc
                op.dcount = dcnt[op.dma]
            elif op.signal:
                cnt[op.eng] += 1
                op.count = cnt[op.eng]
        self.stats = dict(cnt=cnt, ndma_groups=len(dcnt), nops=len(ops))
        with contextlib.ExitStack() as st:
            esem = {s: st.enter_context(nc.semaphore("e_" + s)) for s in self.STREAMS}
            dsem = {g: st.enter_context(nc.semaphore("d_%d" % n)) for n, g in enumerate(dcnt)}
            block = st.enter_context(nc.Block())

            def run(name, eng):
                waited = {}
                for op in ops:
                    if op.eng != name:
                        continue
                    need = {}
                    for j in op.deps:
                        d = ops[j]
                        if d.dma is not None:
                            s, v = dsem[d.dma], d.dcount
                        elif d.eng == op.eng and op.dma is None and (op.eng == "pe" or not SAME_ENGINE_SYNC):
                            continue
                        else:
                            s, v = esem[d.eng], d.count
                        key = id(s)
                        if need.get(key, (None, 0))[1] < v:
                            need[key] = (s, v)
                    for key, (s, v) in need.items():
                        if waited.get(key, 0) < v:
                            eng.wait_ge(s, v)
                            waited[key] = v
                    ins = op.fn(eng)
                    if op.dma is not None:
                        ins.then_inc(dsem[op.dma], op.inc)
                    elif op.signal:
                        ins.then_inc(esem[op.eng], 1)
                if name == "sp":
                    for g, v in dcnt.items():
                        eng.wait_ge(dsem[g], v)

            block.tensor(lambda e: run("pe", e))
            block.scalar(lambda e: run("act", e))
            block.vector(lambda e: run("dve", e))
            block.gpsimd(lambda e: run("pool", e))
            block.sync(lambda e: run("sp", e))


class Arena:
    def __init__(self, nc, st, nbytes, name="arena"):
        self.t = st.enter_context(nc.sbuf_tensor(name, [128, nbytes // 2], BF16))
        self.nbytes = nbytes
        self.off = 0
        self.marks = []

    def alloc(self, shape_free, dtype, parts=128):
        esz = 4 if dtype in (F32, I32) else 2
        n = int(np.prod(shape_free))
        nb = (n * esz + 31) // 32 * 32
        assert self.off + nb <= self.nbytes, ("SBUF arena overflow", self.off, nb, self.nbytes)
        self.hw = max(getattr(self, "hw", 0), self.off + nb)
        a = self.t[0:parts, self.off // 2:(self.off + nb) // 2]
        self.off += nb
        if esz == 4:
            a = a.bitcast(dtype)
        a = a[:, 0:n]
        if len(shape_free) == 2:
            a = a.rearrange("p (a b) -> p a b", b=shape_free[1])
        elif len(shape_free) == 3:
            a = a.rearrange("p (a b c) -> p a b c", b=shape_free[1], c=shape_free[2])
        return a

    def mark(self):
        return self.off

    def reset(self, m):
        self.off = m


class Cfg:
    def __init__(s, D=4096, T=2048, DEPTH=2, NSEQ=1):
        s.D, s.T, s.DEPTH, s.NSEQ = D, T, DEPTH, NSEQ
        NH = D // 128
        s.MLA_H, s.QL, s.KVL, s.ROPE = NH // 4, D // 4, D // 8, 64
        s.SWA_H = NH // 2
        s.SWA_KV = max(1, s.SWA_H // 8)
        s.FOX_H = NH // 4
        s.FFN = ((8 * D // 3 + 255) // 256) * 256
        s.splits = [s.QL, s.KVL, 64, s.SWA_H * 128, s.SWA_KV * 128, s.SWA_KV * 128,
                    s.FOX_H * 128, s.FOX_H * 128, s.FOX_H * 128, s.FOX_H, 3 * D]
        s.INW = sum(s.splits)
        s.KC, s.NT, s.TG = D // 128, T // 128, T // 512
        s.MLA_OUT, s.SWA_OUT, s.FOX_OUT = s.MLA_H * 128, s.SWA_H * 128, s.FOX_H * 128
        s.EPS = 1e-6


PI = float(np.pi)


class Builder:
    def __init__(s, cfg):
        s.c = cfg
        s.nc = bass.Bass("TRN2", target_bir_lowering=False)
        s.P = Prog()
        s.uid = 0
        s.slabctr = 0
        s.halfctr = 0

    def din(s, name, shape, dt=F32):
        return s.nc.dram_tensor(name, list(shape), dt, kind="ExternalInput").ap()

    def dscr(s, name, shape, dt):
        return s.nc.dram_tensor(name, list(shape), dt, kind="Internal").ap()

    def dma(s, q, out, in_, r=(), w=(), grp=None, nowaw=False, **kw):
        if grp is None:
            grp = "misc"
            r = list(r) + ["miscchain"]
            w = list(w) + ["miscchain"]
        s.P.add(q, lambda e: e.dma_start(out=out, in_=in_, **kw), r=r, w=w, dma=grp, nowaw=nowaw)

    def dbg(s, name, ap, shape, dt=F32):
        if not getattr(s.c, "DEBUG", False):
            return
        s.uid += 1
        t = s.nc.dram_tensor("dbg_%s_%d" % (name, s.uid), list(shape), dt, kind="ExternalOutput").ap()
        s.dma("sp", t, ap, r=[name])

    def op(s, eng, fn, r=(), w=()):
        s.P.add(eng, fn, r=r, w=w)

    def build(s):
        c, nc = s.c, s.nc
        D, T, L, NS = c.D, c.T, c.DEPTH, c.NSEQ
        KC, NT, TG = c.KC, c.NT, c.TG
        i = s.inp = {}
        i["xT"] = s.din("xT", [NS, D, T])
        i["posb"] = s.din("posb", [NS, 64, T], I32)
        i["posk"] = s.din("posk", [NS, 128, NT], I32)
        i["posr"] = s.din("posr", [NS, 128, NT], I32)
        i["posrow"] = s.din("posrow", [NS, 16, T], I32)
        i["posrrow"] = s.din("posrrow", [NS, 16, T], I32)
        for g in ("g_mix_pre", "g_mix_post", "g_ffn_pre", "g_ffn_post"):
            i[g] = s.din(g, [L, 128, KC])
        i["g_q_lora"] = s.din("g_q_lora", [L, 128, c.QL // 128])
        i["g_kv_lora"] = s.din("g_kv_lora", [L, 128, c.KVL // 128])
        i["b_forget"] = s.din("b_forget", [L, 128, c.FOX_H])
        i["swa_sinks"] = s.din("swa_sinks", [L, 16, 1])
        i["w_in"] = s.din("w_in", [L, D, c.INW])
        i["w_uq"] = s.din("w_uq", [L, c.QL, c.MLA_H * 192])
        i["w_ukv"] = s.din("w_ukv", [L, c.KVL, c.MLA_H * 256])
        i["w_branch"] = s.din("w_branch", [L, D, D])
        i["w_out"] = s.din("w_out", [L, D, D])
        i["w_gate_up"] = s.din("w_gate_up", [L, D, 2 * c.FFN])
        i["w_down"] = s.din("w_down", [L, c.FFN, D])
        i["cst_f"] = s.din("cst_f", [128, 128 * 4 + 64 + 2])
        i["cst_oh"] = s.din("cst_oh", [16, 16 * 128])
        s.out = nc.dram_tensor("out", [NS, D, T], F32, kind="ExternalOutput").ap()
        sc = s.sc = {}
        sc["cq"] = s.dscr("sc_cq", [c.QL, T], F32)
        sc["ckv"] = s.dscr("sc_ckv", [c.KVL, T], F32)
        sc["kr"] = s.dscr("sc_kr", [64, T], F32)
        sc["z"] = s.dscr("sc_z", [c.FOX_H, T], F32)
        for nm, rows in (("qswa", c.SWA_H * 128), ("kswa", c.SWA_KV * 128), ("vswa", c.SWA_KV * 128),
                         ("qfox", c.FOX_H * 128), ("kfox", c.FOX_H * 128), ("vfox", c.FOX_H * 128),
                         ("gates", 3 * D), ("qn", c.MLA_H * 128), ("qr", c.MLA_H * 64), ("kn", c.MLA_H * 128),
                         ("vm", c.MLA_H * 128), ("krr", 64), ("o", D), ("merged", D), ("a", c.FFN)):
            sc[nm] = s.dscr("sc_" + nm, [rows, T], BF16)
        sc["m"] = s.dscr("sc_m", [D, T], F32)
        sc["wdb"] = s.dscr("sc_wdb", [KC // 2, 128, c.FFN // 128, 256], BF16)
        sc["xa"] = s.dscr("sc_xa", [D, T], F32)
        sc["xb"] = s.dscr("sc_xb", [D, T], F32)
        sc["ssq"] = s.dscr("sc_ssq", [128, T], F32)
        with contextlib.ExitStack() as st:
            s.ar = Arena(nc, st, 206 * 1024)
            s.ps = st.enter_context(nc.psum_tensor("ps", [128, 8, 512], F32))
            s.consts()
            for sq in range(NS):
                s.seq_prep(sq)
                for l in range(L):
                    xin = i["xT"][sq] if l == 0 else sc["xb"]
                    xout = s.out[sq] if l == L - 1 else sc["xb"]
                    s.layer(sq, l, xin, sc["xa"], xout)
            s.P.emit(nc)
        return nc

    def consts(s):
        ar, P, i = s.ar, s.P, s.inp
        cf = ar.alloc([128 * 4 + 66], F32)
        s.dma("sp", cf, i["cst_f"], w=["cf"])
        s.identf = cf[:, 0:128]
        s.U = cf[:, 128:256]
        s.E64 = cf[:, 256:384]
        s.RT = cf[0:64, 512:576]
        s.invf = cf[0:64, 576:577]
        s.slopes = cf[0:16, 577:578]
        s.identb = ar.alloc([128], BF16)
        s.tri = ar.alloc([128], BF16)
        s.low = ar.alloc([128], BF16)
        s.onesb = ar.alloc([128], BF16)
        s.onesf = ar.alloc([128], F32)
        s.op("dve", lambda e: e.tensor_copy(out=s.identb, in_=s.identf), r=["cf"], w=["identb"])
        s.op("dve", lambda e: e.tensor_copy(out=s.tri, in_=cf[:, 384:512]), r=["cf"], w=["tri"])
        s.op("dve", lambda e: e.tensor_scalar(out=s.low, in0=cf[:, 384:512], scalar1=-1.0, scalar2=1.0,
                                              op0=ALU.mult, op1=ALU.add), r=["cf"], w=["low"])
        s.op("dve", lambda e: e.memset(s.onesb, 1.0), w=["onesb"])
        s.op("dve", lambda e: e.memset(s.onesf, 1.0), w=["onesf"])
        s.oh = ar.alloc([16, 128], BF16, parts=16)
        T, NT = s.c.T, s.c.NT
        s.dcur = ar.alloc([NT], F32)
        s.dprev = ar.alloc([NT], F32)
        s.posrel = ar.alloc([T], F32, parts=16)
        s.base = ar.mark()
        ohf = ar.alloc([16 * 128], F32, parts=16)
        s.dma("sp", ohf, i["cst_oh"], w=["ohf"])
        s.op("dve", lambda e: e.tensor_copy(out=s.oh.rearrange("p a b -> p (a b)"), in_=ohf), r=["ohf"], w=["oh"])
        s.sc["cos"] = s.dscr("sc_cos", [64, T], F32)
        s.sc["sin"] = s.dscr("sc_sin", [64, T], F32)
        s.sc["posrel"] = s.dscr("sc_posrel", [16, T], F32)

    def seq_prep(s, sq):
        c, ar, P, i = s.c, s.ar, s.P, s.inp
        T, NT = c.T, c.NT
        P.barrier()
        ar.reset(s.base)
        pi_ = ar.alloc([T], I32, parts=64)
        pf = ar.alloc([T], F32, parts=64)
        ang = ar.alloc([T], F32, parts=64)
        tmp = ar.alloc([T], F32, parts=64)
        s.dma("sp", pi_, i["posb"][sq], w=["pi"])
        s.op("dve", lambda e: e.tensor_copy(out=pf, in_=pi_), r=["pi"], w=["pf"])
        s.op("dve", lambda e: e.tensor_scalar(out=ang, in0=pf, scalar1=s.invf, scalar2=None, op0=ALU.mult),
             r=["pf", "cf"], w=["ang"])
        MAGIC = 12582912.0
        u = ar.alloc([T], F32, parts=64)
        t2 = ar.alloc([T], F32, parts=64)
        s.sinT = ar.alloc([T], F32, parts=64)
        s.cosT = ar.alloc([T], F32, parts=64)
        for tab, sh, nm in ((s.sinT, 0.0, "sinT"), (s.cosT, 0.25, "cosT")):
            s.op("dve", lambda e, sh=sh: e.tensor_scalar(out=u, in0=ang, scalar1=1.0 / (2 * PI), scalar2=sh,
                                                         op0=ALU.mult, op1=ALU.add), r=["ang"], w=["u"])
            s.op("dve", lambda e: e.tensor_scalar(out=tmp, in0=u, scalar1=MAGIC, scalar2=None, op0=ALU.add), r=["u"], w=["tmp"])
            s.op("dve", lambda e: e.tensor_scalar(out=t2, in0=tmp, scalar1=MAGIC, scalar2=None, op0=ALU.subtract), r=["tmp"], w=["t2"])
            s.op("dve", lambda e: e.tensor_tensor(out=u, in0=u, in1=t2, op=ALU.subtract), r=["u", "t2"], w=["u"])
            s.op("act", lambda e, tab=tab: e.activation(out=tab, in_=u, func=AF.Sin, scale=2 * PI), r=["u"], w=[nm])
            s.dbg(nm, tab, [64, T])
            s.dma("sp", s.sc["sin" if nm == "sinT" else "cos"], tab, r=[nm])
        pk = ar.alloc([NT], I32)
        pr = ar.alloc([NT], I32)
        pkf = ar.alloc([NT], F32)
        prf = ar.alloc([NT], F32)
        s.dma("sp", pk, i["posk"][sq], w=["pk"])
        s.dma("sp", pr, i["posr"][sq], w=["pr"])
        s.op("dve", lambda e: e.tensor_copy(out=pkf, in_=pk), r=["pk"], w=["pkf"])
        s.op("dve", lambda e: e.tensor_copy(out=prf, in_=pr), r=["pr"], w=["prf"])
        s.op("dve", lambda e: e.tensor_tensor(out=s.dcur, in0=pkf, in1=prf, op=ALU.subtract), r=["pkf", "prf"], w=["dcur"])
        s.op("dve", lambda e: e.memset(s.dprev[:, 0:1], 0.0), w=["dprev"])
        if NT > 1:
            s.op("dve", lambda e: e.tensor_tensor(out=s.dprev[:, 1:NT], in0=pkf[:, 0:NT - 1], in1=prf[:, 1:NT],
                                                  op=ALU.subtract), r=["pkf", "prf"], w=["dprev"])
        a1 = ar.alloc([T], I32, parts=16)
        a2 = ar.alloc([T], I32, parts=16)
        a1f = ar.alloc([T], F32, parts=16)
        a2f = ar.alloc([T], F32, parts=16)
        s.dma("sp", a1, i["posrow"][sq], w=["a1"])
        s.dma("sp", a2, i["posrrow"][sq], w=["a2"])
        s.op("dve", lambda e: e.tensor_copy(out=a1f, in_=a1), r=["a1"], w=["a1f"])
        s.op("dve", lambda e: e.tensor_copy(out=a2f, in_=a2), r=["a2"], w=["a2f"])
        s.op("dve", lambda e: e.tensor_tensor(out=s.posrel, in0=a1f, in1=a2f, op=ALU.subtract), r=["a1f", "a2f"], w=["posrel"])

    def psr(s, half, m, ntg):
        ap = s.ps[0:m, half * 4:half * 4 + ntg, :].rearrange("p a b -> p (a b)")
        return ap, [("ps", half * 4 + t) for t in range(ntg)]

    def sumsq_rstd(s, src, nch, nfeat, rstd, pre=False):
        c, ar = s.c, s.ar
        T = c.T
        mk = ar.mark()
        if pre:
            ssp, sk = ar.alloc([T], F32), "ssp"
            s.dma("sp", ssp, s.sc["ssq"], w=["ssp"])
            nch = 0
        else:
            xt = [ar.alloc([T], F32) for _ in range(2)]
            sq = ar.alloc([T], F32)
            ssp, sk = ar.alloc([T], F32), "ssp"
        for ch in range(nch):
            b = xt[ch % 2]
            s.dma(Q2[ch % 2], b, src[ch * 128:(ch + 1) * 128, :], w=[("xt", ch % 2)], grp=("xt", ch % 2))
            if ch == 0:
                s.op("act", lambda e, b=b: e.activation(out=ssp, in_=b, func=AF.Square), r=[("xt", 0)], w=["ssp"])
            else:
                s.op("act", lambda e, b=b: e.activation(out=sq, in_=b, func=AF.Square), r=[("xt", ch % 2)], w=["sq"])
                s.op("dve", lambda e: e.tensor_tensor(out=ssp, in0=ssp, in1=sq, op=ALU.add), r=["sq", "ssp"], w=["ssp"])
        for tg in range(c.TG):
            s.op("pe", lambda e, tg=tg: e.matmul(s.ps[:, tg, :], s.onesf, ssp[:, tg * 512:(tg + 1) * 512], start=True, stop=True),
                 r=[sk, "onesf"], w=[("ps", tg)])
        pa, pk = s.psr(0, 128, c.TG)
        s.op("dve", lambda e: e.tensor_scalar(out=rstd, in0=pa, scalar1=1.0 / nfeat, scalar2=c.EPS, op0=ALU.mult, op1=ALU.add),
             r=pk, w=["rstd"])
        s.op("act", lambda e: e.activation(out=rstd, in_=rstd, func=AF.Sqrt), r=["rstd"], w=["rstd"])
        s.op("dve", lambda e: e.reciprocal(out=rstd, in_=rstd), r=["rstd"], w=["rstd"])
        s.P.barrier()
        ar.reset(mk)

    def scale_pass(s, src, nch, gain, rstd, dst):
        c, ar = s.c, s.ar
        mk = ar.mark()
        xt = [ar.alloc([c.T], F32) for _ in range(2)]
        for ch in range(nch):
            b = xt[ch % 2]
            s.dma(Q2[ch % 2], b, src[ch * 128:(ch + 1) * 128, :], w=[("xs", ch % 2)], grp=("xs", ch % 2))
            s.op("dve", lambda e, b=b, ch=ch: e.scalar_tensor_tensor(out=dst[:, ch, :], in0=b, scalar=gain[:, ch:ch + 1], in1=rstd,
                                                                     op0=ALU.mult, op1=ALU.mult),
                 r=[("xs", ch % 2), "rstd", "gain"], w=[("hT", ch)])
        s.P.barrier()
        ar.reset(mk)

    def load_gain(s, src):
        g = s.ar.alloc([src.shape[-1]], F32)
        s.dma("sp", g, src, w=["gain"])
        return g

    def gemm(s, w2d, KC, chunks, rhs, rkeys, evac, SW, ntg, NSL=3, slab_src=None, hook=None):
        ar = s.ar
        wv = w2d.rearrange("(kc p) n -> p kc n", p=128) if w2d is not None else None
        wsl = [ar.alloc([KC, SW], BF16) for _ in range(NSL)]
        slabs, cur, used = [], [], 0
        for ch in chunks:
            if used + ch[1] > SW:
                slabs.append(cur)
                cur, used = [], 0
            cur.append((ch, used))
            used += ch[1]
        if cur:
            slabs.append(cur)
        for si, slab in enumerate(slabs):
            slot = s.slabctr % NSL
            s.slabctr += 1
            rngs = []
            for ch, off in slab:
                if rngs and rngs[-1][0] + rngs[-1][1] == ch[0] and rngs[-1][2] + rngs[-1][1] == off:
                    rngs[-1][1] += ch[1]
                else:
                    rngs.append([ch[0], ch[1], off])
            if hook is not None:
                hook(si)
            if slab_src is not None:
                s.dma("sp", wsl[slot], slab_src(si), w=[("wsl", slot)], grp=("wsl", slot))
                rngs = []
            kmin = min(ch[3] for ch, _ in slab)
            kmax = max(ch[4] for ch, _ in slab)
            if os.environ.get("K_KR", "1") != "1":
                kmin, kmax = 0, KC
            for c0, n, off in rngs:
                s.dma("pool", wsl[slot][:, kmin:kmax, off:off + n], wv[:, kmin:kmax, c0:c0 + n], w=[("wsl", slot)], grp=("wsl", slot),
                      nowaw=(off > 0))
            for ch, off in slab:
                c0, m, tag, k0, k1 = ch
                half = s.halfctr % 2
                s.halfctr += 1
                for kc in range(k0, k1):
                    for tg in range(ntg):
                        s.op("pe", lambda e, o=s.ps[0:m, half * 4 + tg, :], lw=wsl[slot][:, kc, off:off + m], rr=rhs(kc, tg),
                             s0=(kc == k0), s1=(kc == k1 - 1): e.matmul(o, lw, rr, start=s0, stop=s1),
                             r=[("wsl", slot)] + rkeys(kc), w=[("ps", half * 4 + tg)])
                pa, pk = s.psr(half, m, ntg)
                evac(tag, pa, pk, m, half)

    def mk_store(s, name, nslots, Tn, dt, func, q="sp"):
        stg = [s.ar.alloc([Tn], dt) for _ in range(nslots)]
        st = {"i": 0}

        def ev(dst, pa, pk, m, half):
            slot = st["i"] % nslots
            st["i"] += 1
            sb = stg[slot][0:m]
            s.op("act", lambda e: e.activation(out=sb, in_=pa, func=func), r=pk, w=[(name, slot)])
            s.dma(q, dst, sb, r=[(name, slot)], grp=(name, slot))
        return ev

    def mk_store_sq(s, name, nslots, Tn, q, acc):
        stg = [s.ar.alloc([Tn], F32) for _ in range(nslots)]
        sqt = s.ar.alloc([Tn], F32)
        st = {"i": 0}

        def ev(dst, pa, pk, m, half):
            slot = st["i"] % nslots
            first = st["i"] == 0
            st["i"] += 1
            sb = stg[slot]
            s.op("act", lambda e: e.activation(out=sb, in_=pa, func=AF.Copy), r=pk, w=[(name, slot)])
            s.dma(q, dst, sb, r=[(name, slot)], grp=(name, slot))
            if first:
                s.op("act", lambda e: e.activation(out=acc, in_=sb, func=AF.Square), r=[(name, slot)], w=["ssq"])
            else:
                s.op("act", lambda e: e.activation(out=sqt, in_=sb, func=AF.Square), r=[(name, slot)], w=["sqt"])
                s.op("dve", lambda e: e.tensor_tensor(out=acc, in0=acc, in1=sqt, op=ALU.add), r=["sqt", "ssq"], w=["ssq"])
        return ev

    def dump(s, l, sq, names):
        if not getattr(s.c, "DEBUG", False) or l != 0 or sq != 0:
            return
        s.P.barrier()
        for nm in names:
            src = s.sc[nm]
            s.uid += 1
            t = s.nc.dram_tensor("dbg_%s" % nm, list(src.shape), src.dtype, kind="ExternalOutput").ap()
            s.dma("sp", t, src)

    def layer(s, sq, l, xin, xmid, xout):
        s.ph_in(l, xin, pre=(l > 0))
        s.dump(l, sq, ["cq", "ckv", "kr", "qswa", "kswa", "vswa", "qfox", "kfox", "vfox", "z", "gates"])
        s.ph_mla_prep(l)
        s.dump(l, sq, ["qn", "qr", "kn", "vm", "krr"])
        s.ph_attn(l)
        s.dump(l, sq, ["o"])
        s.ph_merge(l)
        s.dump(l, sq, ["merged"])
        s.ph_wout(l)
        s.dump(l, sq, ["m"])
        s.ph_resid(xin, s.inp["g_mix_post"][l], xmid, True)
        s.dump(l, sq, ["xa"])
        s.ph_ffn1(l, xmid)
        s.dump(l, sq, ["a"])
        s.ph_ffn2(l)
        s.ph_resid(xmid, s.inp["g_ffn_post"][l], xout, l < s.c.DEPTH - 1)

    def norm_to_sbuf(s, src, nch, nfeat, gain_src, dst, pre=False):
        ar = s.ar
        g = s.load_gain(gain_src)
        rstd = ar.alloc([s.c.T], F32)
        s.sumsq_rstd(src, nch, nfeat, rstd, pre=pre)
        s.scale_pass(src, nch, g, rstd, dst)

    def ph_in(s, l, xin, pre=False):
        c, ar, P, sc = s.c, s.ar, s.P, s.sc
        T, KC, TG = c.T, c.KC, c.TG
        P.barrier()
        ar.reset(s.base)
        hT = ar.alloc([KC, T], BF16)
        mk = ar.mark()
        s.norm_to_sbuf(xin, KC, c.D, s.inp["g_mix_pre"][l], hT, pre=pre)
        P.barrier()
        ar.reset(mk)
        dsts = [(sc["cq"], "f"), (sc["ckv"], "f"), (sc["kr"], "f"), (sc["qswa"], "b"), (sc["kswa"], "b"), (sc["vswa"], "b"),
                (sc["qfox"], "b"), (sc["kfox"], "b"), (sc["vfox"], "b"), (sc["z"], "f"), (sc["gates"], "s")]
        chunks, col = [], 0
        for (dst, kind), n in zip(dsts, c.splits):
            r0 = 0
            while r0 < n:
                m = min(128, n - r0)
                chunks.append((col + r0, m, (dst[r0:r0 + m, :], kind), 0, KC))
                r0 += m
            col += n
        evf = s.mk_store("stf", 2, T, F32, AF.Copy)
        evb = s.mk_store("stb", 2, T, BF16, AF.Copy)
        evs = s.mk_store("sts", 2, T, BF16, AF.Sigmoid)

        def evac(tag, pa, pk, m, half):
            dst, kind = tag
            {"f": evf, "b": evb, "s": evs}[kind](dst, pa, pk, m, half)
        s.gemm(s.inp["w_in"][l], KC, chunks, lambda kc, tg: hT[:, kc, tg * 512:(tg + 1) * 512],
               lambda kc: [("hT", kc)], evac, 128, TG)

    def rope_store(s, name, xs, xkeys, dst, half):
        c, ar = s.c, s.ar
        T, TG = c.T, c.TG
        t1 = s.rope_t1
        ob = s.rope_ob
        for tg in range(TG):
            s.op("pe", lambda e, tg=tg: e.matmul(s.ps[0:64, half * 4 + tg, :], s.RT, xs[:, tg * 512:(tg + 1) * 512], start=True, stop=True),
                 r=list(xkeys) + ["cf"], w=[("ps", half * 4 + tg)])
        pa, pk = s.psr(half, 64, TG)
        s.op("dve", lambda e: e.tensor_tensor(out=t1, in0=xs, in1=s.cosT, op=ALU.mult), r=list(xkeys) + ["cosT"], w=["rt1"])
        s.op("dve", lambda e: e.tensor_tensor(out=s.rope_t2, in0=pa, in1=s.sinT, op=ALU.mult), r=pk + ["sinT"], w=["rt2"])
        s.op("dve", lambda e: e.tensor_tensor(out=ob, in0=t1, in1=s.rope_t2, op=ALU.add), r=["rt1", "rt2"], w=["rob"])
        s.dma("sp", dst, ob, r=["rob"], grp="rob")

    def ph_mla_prep(s, l):
        c, ar, P, sc, i = s.c, s.ar, s.P, s.sc, s.inp
        T, TG = c.T, c.TG
        QC, VC = c.QL // 128, c.KVL // 128
        P.barrier()
        ar.reset(s.base)
        cqn = ar.alloc([QC, T], BF16)
        ckvn = ar.alloc([VC, T], BF16)
        s.rope_t1 = ar.alloc([T], F32, parts=64)
        s.rope_t2 = ar.alloc([T], F32, parts=64)
        s.rope_ob = ar.alloc([T], BF16, parts=64)
        xs = ar.alloc([T], F32, parts=64)
        s.cosT = ar.alloc([T], F32, parts=64)
        s.sinT = ar.alloc([T], F32, parts=64)
        s.dma("sp", s.cosT, sc["cos"], w=["cosT"])
        s.dma("sp", s.sinT, sc["sin"], w=["sinT"])
        mk = ar.mark()
        s.norm_to_sbuf(sc["cq"], QC, c.QL, i["g_q_lora"][l], cqn)
        P.barrier()
        ar.reset(mk)
        s.norm_to_sbuf(sc["ckv"], VC, c.KVL, i["g_kv_lora"][l], ckvn)
        s.dma("sp", xs, sc["kr"], w=["xs"], grp="xsld")
        s.rope_store("kr", xs, ["xs"], sc["krr"], 1)
        P.barrier()
        ar.reset(mk)
        evb = s.mk_store("stb", 2, T, BF16, AF.Copy)
        chunks = []
        for h in range(c.MLA_H):
            chunks.append((h * 192, 128, ("n", sc["qn"][h * 128:(h + 1) * 128, :]), 0, QC))
            chunks.append((h * 192 + 128, 64, ("r", sc["qr"][h * 64:(h + 1) * 64, :]), 0, QC))

        def evq(tag, pa, pk, m, half):
            kind, dst = tag
            if kind == "n":
                evb(dst, pa, pk, m, half)
            else:
                s.op("act", lambda e: e.activation(out=xs, in_=pa, func=AF.Copy), r=pk, w=["xs"])
                s.rope_store("qr", xs, ["xs"], dst, half)
        s.gemm(i["w_uq"][l], QC, chunks, lambda kc, tg: cqn[:, kc, tg * 512:(tg + 1) * 512],
               lambda kc: [("hT", kc)], evq, 192, TG)
        P.barrier()
        chunks = []
        for h in range(c.MLA_H):
            chunks.append((h * 256, 128, sc["kn"][h * 128:(h + 1) * 128, :], 0, VC))
            chunks.append((h * 256 + 128, 128, sc["vm"][h * 128:(h + 1) * 128, :], 0, VC))
        s.gemm(i["w_ukv"][l], VC, chunks, lambda kc, tg: ckvn[:, kc, tg * 512:(tg + 1) * 512],
               lambda kc: [("hT", kc)], evb, 256, TG)

    def ph_attn(s, l):
        c, ar, P, sc, i = s.c, s.ar, s.P, s.sc, s.inp
        T, NT, FH, SH = c.T, c.NT, c.FOX_H, c.SWA_H
        P.barrier()
        ar.reset(s.base)
        zT = ar.alloc([T], F32, parts=FH)
        bfor = ar.alloc([FH], F32)
        zt = ar.alloc([NT, FH], F32)
        Fp = ar.alloc([NT, FH], F32)
        off = ar.alloc([NT, FH], F32)
        Fref = ar.alloc([NT, FH], F32)
        biasF = ar.alloc([FH, NT, NT], F32)
        s.dma("sp", zT, sc["z"], w=["zT"])
        s.dma("sp", bfor, i["b_forget"][l], w=["bfor"])
        pz = s.ps[:, 7, 0:NT * FH].rearrange("p (a b) -> p a b", b=FH)
        for j in range(NT):
            s.op("pe", lambda e, j=j: e.transpose(out=s.ps[:, 7, j * FH:(j + 1) * FH], in_=zT[0:FH, j * 128:(j + 1) * 128],
                                                  identity=s.identf[0:FH, 0:FH]), r=["zT", "cf"], w=[("ps", 7)])
        for j in range(NT):
            s.op("dve", lambda e, j=j: e.tensor_tensor(out=zt[:, j, :], in0=pz[:, j, :], in1=bfor, op=ALU.add),
                 r=[("ps", 7), "bfor"], w=["zt"])
        ztf = zt.rearrange("p a b -> p (a b)")
        s.op("act", lambda e: e.activation(out=ztf, in_=ztf, func=AF.Exp, scale=-1.0), r=["zt"], w=["zt"])
        s.op("act", lambda e: e.activation(out=ztf, in_=ztf, func=AF.Ln, bias=s.onesf[:, 0:1], scale=1.0), r=["zt", "onesf"], w=["zt"])
        n = NT * FH
        s.op("pe", lambda e: e.matmul(s.ps[:, 6, 0:n], s.U, ztf, start=True, stop=True), r=["zt", "cf"], w=[("ps", 6)])
        s.op("pe", lambda e: e.matmul(s.ps[:, 5, 0:n], s.onesf, ztf, start=True, stop=True), r=["zt", "onesf"], w=[("ps", 5)])
        pt = s.ps[:, 5, 0:n].rearrange("p (a b) -> p a b", b=FH)
        s.op("dve", lambda e: e.memset(off[:, 0, :], 0.0), w=["off"])
        for j in range(1, NT):
            s.op("dve", lambda e, j=j: e.tensor_tensor(out=off[:, j, :], in0=off[:, j - 1, :], in1=pt[:, j - 1, :], op=ALU.add),
                 r=[("ps", 5), "off"], w=["off"])
        s.op("dve", lambda e: e.tensor_tensor(out=Fp.rearrange("p a b -> p (a b)"), in0=s.ps[:, 6, 0:n],
                                              in1=off.rearrange("p a b -> p (a b)"), op=ALU.add), r=[("ps", 6), "off"], w=["Fp"])
        s.op("pe", lambda e: e.matmul(s.ps[:, 7, 0:n], s.E64, Fp.rearrange("p a b -> p (a b)"), start=True, stop=True),
             r=["Fp", "cf"], w=[("ps", 7)])
        s.op("dve", lambda e: e.tensor_copy(out=Fref.rearrange("p a b -> p (a b)"), in_=s.ps[:, 7, 0:n]), r=[("ps", 7)], w=["Fref"])
        for h in range(FH):
            for kt in range(NT):
                s.op("dve", lambda e, h=h, kt=kt: e.tensor_scalar(out=biasF[:, h, kt, :], in0=Fref[:, :, h], scalar1=Fp[:, kt, h:h + 1],
                                                                  scalar2=-1.0, op0=ALU.subtract, op1=ALU.mult),
                     r=["Fref", "Fp"], w=["biasF"])
        slopes = [2.0 ** (-8.0 * (h + 1) / SH) for h in range(SH)]
        biasS = ar.alloc([SH, NT, 2], F32)
        for h in range(SH):
            s.op("dve", lambda e, h=h: e.tensor_scalar(out=biasS[:, h, :, 0], in0=s.dcur, scalar1=slopes[h], scalar2=None, op0=ALU.mult),
                 r=["dcur"], w=["biasS"])
            s.op("dve", lambda e, h=h: e.tensor_scalar(out=biasS[:, h, :, 1], in0=s.dprev, scalar1=slopes[h], scalar2=None, op0=ALU.mult),
                 r=["dprev"], w=["biasS"])
        sinks = ar.alloc([1], F32, parts=16)
        s.dma("sp", sinks, i["swa_sinks"][l], w=["sinks"])

        sinkT = ar.alloc([T], BF16, parts=16)
        s.op("act", lambda e: e.activation(out=sinkT[0:SH], in_=s.posrel[0:SH], func=AF.Exp, bias=sinks[0:SH], scale=s.slopes[0:SH]),
             r=["posrel", "sinks", "cf"], w=["sinkT"])
        s.sinkT = sinkT
        s.dbg("sinkT", sinkT[0:SH], [SH, T], BF16)
        s.dbg("biasS", biasS.rearrange("p a b c -> p (a b c)"), [128, SH * NT * 2])
        s.dbg("biasF", biasF.rearrange("p a b c -> p (a b c)"), [128, FH * NT * NT])
        s.dbg("Fp", Fp.rearrange("p a b -> p (a b)"), [128, NT * FH])
        s.dbg("zt", ztf, [128, NT * FH])
        NB = 2
        s.hb = []
        for b in range(NB):
            s.hb.append(dict(q=ar.alloc([T], BF16), k=ar.alloc([T], BF16), v=ar.alloc([T], BF16), vt=ar.alloc([NT, 128], BF16),
                             q2=ar.alloc([T], BF16, parts=64)))
        s.k2 = ar.alloc([T], BF16, parts=64)
        s.PT = [ar.alloc([128], BF16) for _ in range(4)]
        s.rinv = ar.alloc([512], F32)
        s.ob = [ar.alloc([512], BF16) for _ in range(2)]
        s.hctr = 0
        s.sctr = 0
        s.pctr = 0
        s.gctr = 0
        s.dma("sp", s.k2, sc["krr"], w=["k2"])
        causal = lambda qb: [(kt, "tri" if kt == qb else None) for kt in range(qb + 1)]
        banded = lambda qb: ([(qb - 1, "low")] if qb > 0 else []) + [(qb, "tri")]
        o = sc["o"]
        heads = []
        for h in range(c.MLA_H):
            heads.append((sc["qn"][h * 128:(h + 1) * 128], sc["kn"][h * 128:(h + 1) * 128], sc["vm"][h * 128:(h + 1) * 128],
                          sc["qr"][h * 64:(h + 1) * 64], 192.0 ** -0.5, lambda kt, qb, rel: 0.0, causal, None, o[h * 128:(h + 1) * 128]))
        G = SH // c.SWA_KV
        for h in range(SH):
            kv = h // G
            heads.append((sc["qswa"][h * 128:(h + 1) * 128], sc["kswa"][kv * 128:(kv + 1) * 128], sc["vswa"][kv * 128:(kv + 1) * 128],
                          None, 128.0 ** -0.5, lambda kt, qb, rel, h=h: biasS[:, h, qb, rel:rel + 1], banded, h,
                          o[c.MLA_OUT + h * 128:c.MLA_OUT + (h + 1) * 128]))
        for h in range(FH):
            r0 = c.MLA_OUT + c.SWA_OUT + h * 128
            heads.append((sc["qfox"][h * 128:(h + 1) * 128], sc["kfox"][h * 128:(h + 1) * 128], sc["vfox"][h * 128:(h + 1) * 128],
                          None, 128.0 ** -0.5, lambda kt, qb, rel, h=h: biasF[:, h, kt, qb:qb + 1], causal, None, o[r0:r0 + 128]))
        s.attn_load(heads[0], 0)
        for hi, hd in enumerate(heads):
            if hi + 1 < len(heads):
                s.attn_load(heads[hi + 1], (hi + 1) % 2)
            s.attn_head(hi % 2, *hd)

    def attn_load(s, hd, b):
        qsrc, ksrc, vsrc, q2src = hd[0:4]
        hb = s.hb[b]
        kq, kk, kv_, kq2 = ("hq", b), ("hk", b), ("hv", b), ("hq2", b)
        s.dma("sp", hb["q"], qsrc, w=[kq], grp=kq)
        s.dma("sp", hb["k"], ksrc, w=[kk], grp=kk)
        s.dma("sp", hb["v"], vsrc, w=[kv_], grp=kv_)
        if q2src is not None:
            s.dma("sp", hb["q2"][0:64], q2src, w=[kq2], grp=kq2)

    def attn_head(s, b, qsrc, ksrc, vsrc, q2src, scale, bias_fn, keytiles, sink, dst):
        c = s.c
        T, NT = c.T, c.NT
        hb = s.hb[b]
        kq, kk, kv_, kvt, kq2 = ("hq", b), ("hk", b), ("hv", b), ("hvt", b), ("hq2", b)
        SBANK = (0, 1, 2, 3)
        for j0 in range(0, NT, 8):
            nb = min(8, NT - j0)
            bank = (j0 // 8) % 2
            pv = s.ps[:, bank, :].bitcast(BF16)
            for j in range(nb):
                s.op("pe", lambda e, j=j, j0=j0, pv=pv: e.transpose(out=pv[:, j * 128:(j + 1) * 128], in_=hb["v"][:, (j0 + j) * 128:(j0 + j + 1) * 128],
                                                                    identity=s.identb), r=[kv_, "identb"], w=[("ps", bank)])
            s.op("act", lambda e, j0=j0, nb=nb, pv=pv: e.activation(out=hb["vt"][:, j0:j0 + nb, :].rearrange("p a b -> p (a b)"),
                                                                    in_=pv[:, 0:nb * 128], func=AF.Copy), r=[("ps", bank)], w=[kvt])
        tiles = []
        for qb in range(NT):
            kl = keytiles(qb)
            for ti, (kt, mt) in enumerate(kl):
                tiles.append((qb, kt, mt, ti == 0, ti == len(kl) - 1))

        def emit_S(t):
            qb, kt, mt, first, last = t
            sb = s.sctr % 4
            s.sctr += 1
            o_ = s.ps[:, SBANK[sb], 0:128]
            s.op("pe", lambda e: e.matmul(o_, hb["k"][:, kt * 128:(kt + 1) * 128], hb["q"][:, qb * 128:(qb + 1) * 128],
                                          start=True, stop=(q2src is None)), r=[kk, kq], w=[("ps", SBANK[sb])])
            if q2src is not None:
                s.op("pe", lambda e: e.matmul(o_, s.k2[0:64, kt * 128:(kt + 1) * 128], hb["q2"][0:64, qb * 128:(qb + 1) * 128],
                                              start=False, stop=True), r=["k2", kq2], w=[("ps", SBANK[sb])])
            return sb

        def emit_PV(t, sb):
            qb, kt, mt, first, last = t
            qg, qi = qb // 4, qb % 4
            ot, rb = 4 + qg % 2, 6 + qg % 2
            cols = slice(qi * 128, (qi + 1) * 128)
            rcols = cols
            ps_ = s.pctr % 4
            s.pctr += 1
            sin_ = s.ps[:, SBANK[sb], 0:128]
            PT = s.PT[ps_]
            rel = 0 if kt == qb else 1
            bias = bias_fn(kt, qb, rel)
            bk = [] if isinstance(bias, float) else ["biasF", "biasS"]
            s.op("act", lambda e: e.activation(out=PT, in_=sin_, func=AF.Exp, bias=bias, scale=scale),
                 r=[("ps", SBANK[sb])] + bk, w=[("PT", ps_)])
            if mt is not None:
                mk = s.tri if mt == "tri" else s.low
                s.op("pool", lambda e: e.tensor_tensor(out=PT, in0=PT, in1=mk, op=ALU.mult), r=[("PT", ps_), mt], w=[("PT", ps_)])
            s.op("pe", lambda e: e.matmul(s.ps[:, ot, cols], hb["vt"][:, kt, :], PT, start=first, stop=last),
                 r=[kvt, ("PT", ps_)], w=[("ps", ot)])
            s.op("pe", lambda e: e.matmul(s.ps[:, rb, rcols], s.onesb, PT, start=first, stop=(last and sink is None)),
                 r=["onesb", ("PT", ps_)], w=[("ps", rb)])
            if last and sink is not None:
                SH = c.SWA_H
                s.op("pe", lambda e: e.matmul(s.ps[:, rb, rcols], s.oh[0:SH, sink, :], s.sinkT[0:SH, qb * 128:(qb + 1) * 128],
                                              start=False, stop=True), r=["oh", "sinkT"], w=[("ps", rb)])
            if last and qi == 3:
                gslot = s.gctr % 2
                s.gctr += 1
                s.op("dve", lambda e: e.reciprocal(out=s.rinv, in_=s.ps[:, rb, :]), r=[("ps", rb)], w=["rinv"])
                s.op("dve", lambda e: e.tensor_tensor(out=s.ob[gslot], in0=s.ps[:, ot, :], in1=s.rinv, op=ALU.mult),
                     r=[("ps", ot), "rinv"], w=[("ob", gslot)])
                s.dma("sp", dst[:, qg * 512:(qg + 1) * 512], s.ob[gslot], r=[("ob", gslot)], grp=("ob", gslot))

        pend = []
        for t in tiles:
            pend.append((t, emit_S(t)))
            if len(pend) > int(os.environ.get("K_LA", "3")):
                emit_PV(*pend.pop(0))
        while pend:
            emit_PV(*pend.pop(0))


    def load_fm(s, src, nch, dst):
        for ch in range(nch):
            s.dma("sp", dst[:, ch, :], src[ch * 128:(ch + 1) * 128, :], w=["hTall"], grp="ldfm", nowaw=True)

    def ph_merge(s, l):
        c, ar, P, sc, i = s.c, s.ar, s.P, s.sc, s.inp
        T, KC, TG, D = c.T, c.KC, c.TG, c.D
        P.barrier()
        ar.reset(s.base)
        oT = ar.alloc([KC, T], BF16)
        s.load_fm(sc["o"], KC, oT)
        gt = [[ar.alloc([T], BF16) for _ in range(3)]] * 2
        acc = ar.alloc([T], F32)
        tmp = ar.alloc([T], F32)
        mst = [ar.alloc([T], BF16) for _ in range(2)]
        kA, kB = c.MLA_OUT // 128, (c.MLA_OUT + c.SWA_OUT) // 128
        kr = [(0, kA), (kA, kB), (kB, KC)]
        chunks = []
        for cc in range(KC):
            for br in range(3):
                chunks.append((cc * 128, 128, (cc, br), kr[br][0], kr[br][1]))

        def evac(tag, pa, pk, m, half):
            cc, br = tag
            gs = cc % 2
            if br == 0:
                for b3 in range(3):
                    s.dma("sp", gt[0][b3], sc["gates"][b3 * D + cc * 128:b3 * D + (cc + 1) * 128, :], w=[("gt", 0, b3)], grp=("gt", 0, b3))
                s.op("dve", lambda e: e.tensor_tensor(out=acc, in0=pa, in1=gt[gs][0], op=ALU.mult), r=pk + [("gt", 0, 0)], w=["acc"])
            elif br == 1:
                s.op("dve", lambda e: e.tensor_tensor(out=tmp, in0=pa, in1=gt[gs][1], op=ALU.mult), r=pk + [("gt", 0, 1)], w=["tmp"])
                s.op("dve", lambda e: e.tensor_tensor(out=acc, in0=acc, in1=tmp, op=ALU.add), r=["acc", "tmp"], w=["acc"])
            else:
                s.op("dve", lambda e: e.tensor_tensor(out=tmp, in0=pa, in1=gt[gs][2], op=ALU.mult), r=pk + [("gt", 0, 2)], w=["tmp"])
                s.op("dve", lambda e: e.tensor_tensor(out=mst[gs], in0=acc, in1=tmp, op=ALU.add), r=["acc", "tmp"], w=[("mst", gs)])
                s.dma("sp", sc["merged"][cc * 128:(cc + 1) * 128, :], mst[gs], r=[("mst", gs)], grp=("mst", gs))
        s.gemm(i["w_branch"][l], KC, chunks, lambda kc, tg: oT[:, kc, tg * 512:(tg + 1) * 512],
               lambda kc: ["hTall"], evac, 128, TG)

    def ph_wout(s, l):
        c, ar, P, sc, i = s.c, s.ar, s.P, s.sc, s.inp
        T, KC, TG = c.T, c.KC, c.TG
        P.barrier()
        ar.reset(s.base)
        mT = ar.alloc([KC, T], BF16)
        s.load_fm(sc["merged"], KC, mT)
        ssq = ar.alloc([T], F32)
        evf = s.mk_store_sq("stf", 2, T, "sp", ssq)
        chunks = [(cc * 128, 128, sc["m"][cc * 128:(cc + 1) * 128, :], 0, KC) for cc in range(KC)]
        s.gemm(i["w_out"][l], KC, chunks, lambda kc, tg: mT[:, kc, tg * 512:(tg + 1) * 512],
               lambda kc: ["hTall"], evf, 128, TG)
        s.dma("sp", sc["ssq"], ssq, r=["ssq"])

    def ph_resid(s, xsrc, gain_src, xdst, want_ssq):
        c, ar, P, sc = s.c, s.ar, s.P, s.sc
        T, KC = c.T, c.KC
        P.barrier()
        ar.reset(s.base)
        g = s.load_gain(gain_src)
        rstd = ar.alloc([T], F32)
        s.sumsq_rstd(sc["m"], KC, c.D, rstd, pre=True)
        mt = [ar.alloc([T], F32) for _ in range(2)]
        xt = [ar.alloc([T], F32) for _ in range(2)]
        sqt = ar.alloc([T], F32)
        ssq = ar.alloc([T], F32)
        for ch in range(KC):
            b = ch % 2
            rows = slice(ch * 128, (ch + 1) * 128)
            s.dma("sp", mt[b], sc["m"][rows, :], w=[("rm", b)], grp=("rm", b))
            s.dma(Q2[1], xt[b], xsrc[rows, :], w=[("rx", b)], grp=("rx", b))
            s.op("dve", lambda e, b=b, ch=ch: e.scalar_tensor_tensor(out=mt[b], in0=mt[b], scalar=g[:, ch:ch + 1], in1=rstd,
                                                                     op0=ALU.mult, op1=ALU.mult), r=[("rm", b), "rstd", "gain"], w=[("rm", b)])
            s.op("pool", lambda e, b=b: e.tensor_tensor(out=xt[b], in0=xt[b], in1=mt[b], op=ALU.add), r=[("rm", b), ("rx", b)], w=[("rx", b)])
            if want_ssq and ch == 0:
                s.op("act", lambda e, b=b: e.activation(out=ssq, in_=xt[b], func=AF.Square), r=[("rx", b)], w=["ssq"])
            elif want_ssq:
                s.op("act", lambda e, b=b: e.activation(out=sqt, in_=xt[b], func=AF.Square), r=[("rx", b)], w=["sqt"])
                s.op("dve", lambda e: e.tensor_tensor(out=ssq, in0=ssq, in1=sqt, op=ALU.add), r=["sqt", "ssq"], w=["ssq"])
            s.dma(Q2[b], xdst[rows, :], xt[b], r=[("rx", b)], grp=("rxs", b))
        if want_ssq:
            s.dma("sp", sc["ssq"], ssq, r=["ssq"])

    def ph_ffn1(s, l, xmid):
        c, ar, P, sc, i = s.c, s.ar, s.P, s.sc, s.inp
        T, KC, TG = c.T, c.KC, c.TG
        P.barrier()
        ar.reset(s.base)
        hT = ar.alloc([KC, T], BF16)
        mk = ar.mark()
        s.norm_to_sbuf(xmid, KC, c.D, i["g_ffn_pre"][l], hT, pre=True)
        P.barrier()
        ar.reset(mk)
        sg = [ar.alloc([T], F32) for _ in range(2)]
        ast = [ar.alloc([T], BF16) for _ in range(2)]
        NJ = c.FFN // 128
        chunks = []
        for j in range(NJ):
            chunks.append((j * 128, 128, (j, 0), 0, KC))
            chunks.append((c.FFN + j * 128, 128, (j, 1), 0, KC))

        def evac(tag, pa, pk, m, half):
            j, u = tag
            b = j % 2
            if u == 0:
                s.op("act", lambda e: e.activation(out=sg[b], in_=pa, func=AF.Silu), r=pk, w=[("sg", b)])
            else:
                s.op("dve", lambda e: e.tensor_tensor(out=ast[b], in0=pa, in1=sg[b], op=ALU.mult), r=pk + [("sg", b)], w=[("ast", b)])
                s.dma("sp", sc["a"][j * 128:(j + 1) * 128, :], ast[b], r=[("ast", b)], grp=("ast", b))
        NP = KC // 2
        wd = i["w_down"][l].rearrange("(kc p) n -> p kc n", p=128)
        nsl_tot = 2 * NJ
        step = max(1, nsl_tot // (NP + 1))

        def hook(si):
            if si % step == 0 and si // step < NP:
                pr = si // step
                s.dma("pool", sc["wdb"][pr], wd[:, :, pr * 256:(pr + 1) * 256], grp=("wdb", pr % 2))
        s.gemm(i["w_gate_up"][l], KC, chunks, lambda kc, tg: hT[:, kc, tg * 512:(tg + 1) * 512],
               lambda kc: [("hT", kc)], evac, 128, TG, NSL=4, hook=hook)

    def ph_ffn2(s, l):
        c, ar, P, sc, i = s.c, s.ar, s.P, s.sc, s.inp
        T, KC, TG = c.T, c.KC, c.TG
        NJ = c.FFN // 128
        for tq in range(TG):
            P.barrier()
            ar.reset(s.base)
            aT = ar.alloc([NJ, 512], BF16)
            for j in range(NJ):
                s.dma("sp", aT[:, j, :], sc["a"][j * 128:(j + 1) * 128, tq * 512:(tq + 1) * 512], w=["hTall"], grp="ldfm", nowaw=True)
            ssq = ar.alloc([512], F32)
            evf = s.mk_store_sq("stf", 3, 512, "act", ssq)
            chunks = [(cc * 128, 128, sc["m"][cc * 128:(cc + 1) * 128, tq * 512:(tq + 1) * 512], 0, NJ) for cc in range(KC)]
            s.gemm(None, NJ, chunks, lambda kc, tg: aT[:, kc, :], lambda kc: ["hTall"], evf, 256, 1, NSL=2,
                   slab_src=lambda si: sc["wdb"][si])
            s.dma("sp", sc["ssq"][:, tq * 512:(tq + 1) * 512], ssq, r=["ssq"])


def make_consts(cfg):
    cf = np.zeros((128, 578), np.float32)
    cf[:, 0:128] = np.eye(128, dtype=np.float32)
    cf[:, 128:256] = np.triu(np.ones((128, 128), np.float32))
    cf[64, 256:384] = 1.0
    cf[:, 384:512] = np.triu(np.ones((128, 128), np.float32))
    RT = np.zeros((64, 64), np.float32)
    for m in range(32):
        RT[m + 32, m] = -1.0
        RT[m, m + 32] = 1.0
    cf[0:64, 512:576] = RT
    half = 32
    inv = (10000.0 ** (-np.arange(half, dtype=np.float32) / half)).astype(np.float32)
    cf[0:64, 576] = np.concatenate([inv, inv])
    SH = cfg.SWA_H
    cf[0:SH, 577] = (2.0 ** (-8.0 * np.arange(1, SH + 1, dtype=np.float32) / SH)).astype(np.float32)
    oh = np.zeros((16, 16, 128), np.float32)
    for h in range(16):
        oh[h, h, :] = 1.0
    return cf, oh.reshape(16, 16 * 128)


def make_in_map(cfg, seqs, inputs):
    c = cfg
    T, NT = c.T, c.NT
    x = inputs["x"]
    pos = np.asarray(inputs["positions"]).astype(np.int32)
    m = {}
    m["xT"] = np.ascontiguousarray(np.stack([x[b].T for b in seqs]))
    p = np.stack([pos[b] for b in seqs])
    m["posb"] = np.ascontiguousarray(np.broadcast_to(p[:, None, :], (len(seqs), 64, T)))
    pk = p.reshape(len(seqs), NT, 128).transpose(0, 2, 1)
    m["posk"] = np.ascontiguousarray(pk)
    pr = p.reshape(len(seqs), NT, 128)[:, :, 64]
    m["posr"] = np.ascontiguousarray(np.broadcast_to(pr[:, None, :], (len(seqs), 128, NT)))
    m["posrow"] = np.ascontiguousarray(np.broadcast_to(p[:, None, :], (len(seqs), 16, T)))
    prr = np.repeat(pr, 128, axis=1)
    m["posrrow"] = np.ascontiguousarray(np.broadcast_to(prr[:, None, :], (len(seqs), 16, T)))

    def fm(g, n):
        return np.ascontiguousarray(g.reshape(g.shape[0], n // 128, 128).transpose(0, 2, 1))
    for g in ("g_mix_pre", "g_mix_post", "g_ffn_pre", "g_ffn_post"):
        m[g] = fm(inputs[g], c.D)
    m["g_q_lora"] = fm(inputs["g_q_lora"], c.QL)
    m["g_kv_lora"] = fm(inputs["g_kv_lora"], c.KVL)
    bf = inputs["b_forget"]
    m["b_forget"] = np.ascontiguousarray(np.broadcast_to(bf[:, None, :], (bf.shape[0], 128, bf.shape[1])))
    sk = np.zeros((inputs["swa_sinks"].shape[0], 16, 1), np.float32)
    sk[:, 0:c.SWA_H, 0] = inputs["swa_sinks"]
    m["swa_sinks"] = sk
    for w in ("w_in", "w_uq", "w_ukv", "w_branch", "w_out", "w_gate_up", "w_down"):
        m[w] = inputs[w]
    cf, oh = make_consts(c)
    m["cst_f"] = cf
    m["cst_oh"] = oh
    return m


NCORES = 8
_CACHE = {}


def run(cfg, ncores, inputs, trace=False):
    B = inputs["x"].shape[0]
    assert B == ncores * cfg.NSEQ
    key = (cfg.D, cfg.T, cfg.DEPTH, cfg.NSEQ)
    if key not in _CACHE:
        _CACHE[key] = Builder(cfg).build()
    nc = _CACHE[key]
    in_maps = [make_in_map(cfg, list(range(cid * cfg.NSEQ, (cid + 1) * cfg.NSEQ)), inputs) for cid in range(ncores)]
    res = run_bass_kernel_spmd(nc, in_maps, core_ids=list(range(ncores)), **({"trace": True} if trace else {}))
    outs = []
    for cid in range(ncores):
        o = res.results[cid]["out"]
        for j in range(cfg.NSEQ):
            outs.append(o[j].T)
    return np.ascontiguousarray(np.stack(outs)).astype(np.float32), res


def kernel(**inputs):
    inputs = {k: np.asarray(v) for k, v in inputs.items()}
    cfg = Cfg(D=4096, T=2048, DEPTH=2, NSEQ=8 // NCORES)
    out, _ = run(cfg, NCORES, inputs)
    return out
```
